# Optimizing a Trainium2 kernel written in Bass

```python
import jax, jax.numpy as jnp
from jax import lax
import numpy as np

D_MODEL = 1024
BATCH = 4
SEQ = 8192
DEPTH = 4

CHUNK = 64
GLA_HEADS = 4
GLA_DK = D_MODEL // 2
GLA_DV = D_MODEL
GLA_HEAD_DK = GLA_DK // GLA_HEADS
GLA_HEAD_DV = GLA_DV // GLA_HEADS
GATE_RANK = 16
GATE_TAU = 16.0
POOL_WIDTH = D_MODEL
POOL_WINDOWS = (2, 4, 8, 16)
POOL_GROUPS = len(POOL_WINDOWS)
POOL_GROUP_DIM = POOL_WIDTH // POOL_GROUPS
N_BRANCHES = 2
IN_SPLITS = (GLA_DK, GLA_DK, GLA_DV, GLA_DV, GATE_RANK, POOL_WIDTH, POOL_WIDTH, N_BRANCHES * D_MODEL)
IN_COLS = sum(IN_SPLITS)
IN_OFFSETS = tuple(int(v) for v in np.cumsum(IN_SPLITS)[:-1])
DEEPNORM_ALPHA = (2.0 * DEPTH) ** 0.25
DEEPNORM_BETA = (8.0 * DEPTH) ** -0.25
EPS = 1e-5

kernel_name = "hybrid_gla_pool_deepnorm_encoder"


def _layernorm(x, g, b):
    x32 = x.astype(jnp.float32)
    mu = jnp.mean(x32, axis=-1, keepdims=True)
    var = jnp.mean(jnp.square(x32 - mu), axis=-1, keepdims=True)
    return ((x32 - mu) * lax.rsqrt(var + EPS) * g + b).astype(x.dtype)


def _gla_chunked(q, k, v, log_alpha):
    B, S, H, dk = q.shape
    dv = v.shape[-1]
    nc = S // CHUNK

    def to_chunks(a):
        return a.reshape(B, nc, CHUNK, H, a.shape[-1]).transpose(1, 0, 3, 2, 4)

    qc, kc, vc, lac = to_chunks(q), to_chunks(k), to_chunks(v), to_chunks(log_alpha)

    def step(state, inp):
        qb, kb, vb, lab = inp
        G = jnp.cumsum(lab, axis=2)
        decay = jnp.exp(-jnp.abs(G[:, :, :, None, :] - G[:, :, None, :, :]))
        scores = jnp.einsum('bhtd,bhsd,bhtsd->bhts', qb, kb, decay)
        o_intra = jnp.einsum('bhts,bhsv->bhtv', scores, vb)
        o_inter = jnp.einsum('bhtd,bhdv->bhtv', qb * jnp.exp(G), state)
        G_last = G[:, :, -1:, :]
        new_state = jnp.exp(G_last)[:, :, 0, :, None] * state + jnp.einsum(
            'bhsd,bhsv->bhdv', kb * jnp.exp(G_last - G), vb)
        return new_state, o_intra + o_inter

    state0 = jnp.zeros((B, H, dk, dv), jnp.float32)
    _, o = lax.scan(step, state0, (qc, kc, vc, lac))
    return o.transpose(1, 0, 3, 2, 4).reshape(B, S, H, dv)


def _multiscale_pool(u, w_grp, scale):
    B, S, _ = u.shape
    ug = u.astype(jnp.float32).reshape(B, S, POOL_GROUPS, POOL_GROUP_DIM)
    csum = jnp.cumsum(ug, axis=1)
    pos = jnp.arange(1, S + 1)
    means = []
    for g, w in enumerate(POOL_WINDOWS):
        c = csum[:, :, g]
        shifted = jnp.pad(c, ((0, 0), (w, 0), (0, 0)))[:, :S]
        cnt = jnp.minimum(pos, w).astype(jnp.float32)
        means.append((c - shifted) / cnt[None, :, None])
    pooled = jnp.stack(means, axis=2) - ug
    mixed = jnp.einsum('bsgi,gio->bsgo', pooled, w_grp)
    return mixed.reshape(B, S, POOL_WIDTH) * scale


def _layer(x, w_in, w_alpha_up, b_alpha, gla_norm_g, w_pool_grp, pool_scale,
           b_merge, w_proj_a, w_proj_b, w_out, ln_g, ln_b):
    B, S, _ = x.shape
    h = x @ w_in
    q, k, v, gate_a, alpha_low, pool_in, gate_b, merge_logits = jnp.split(h, IN_OFFSETS, axis=-1)

    log_alpha = jax.nn.log_sigmoid((alpha_low @ w_alpha_up + b_alpha).astype(jnp.float32)) / GATE_TAU
    qh = q.reshape(B, S, GLA_HEADS, GLA_HEAD_DK) * (GLA_HEAD_DK ** -0.5)
    kh = k.reshape(B, S, GLA_HEADS, GLA_HEAD_DK)
    vh = v.reshape(B, S, GLA_HEADS, GLA_HEAD_DV)
    lah = log_alpha.reshape(B, S, GLA_HEADS, GLA_HEAD_DK)
    o = _gla_chunked(qh, kh, vh, lah)
    o = o * lax.rsqrt(jnp.mean(jnp.square(o), axis=-1, keepdims=True) + EPS) * gla_norm_g
    y_a = o.reshape(B, S, GLA_DV).astype(x.dtype) * jax.nn.silu(gate_a)

    y_b = _multiscale_pool(pool_in, w_pool_grp, pool_scale).astype(x.dtype) * jax.nn.silu(gate_b)

    gates = jax.nn.sigmoid(merge_logits + b_merge)
    g_a, g_b = jnp.split(gates, N_BRANCHES, axis=-1)
    merged = g_a * (y_a @ w_proj_a) + g_b * (y_b @ w_proj_b)
    y = merged @ w_out

    return _layernorm(DEEPNORM_ALPHA * x + y, ln_g, ln_b)


def setup_inputs(seed: int = 0) -> dict:
    key = jax.random.key(seed)
    ks = jax.random.split(key, 14)
    f32 = jnp.float32
    nrm = lambda k, shape, s: jax.random.normal(k, shape, f32) * s
    return {
        "x": jax.random.normal(ks[0], (BATCH, SEQ, D_MODEL), f32),
        "w_in": nrm(ks[1], (DEPTH, D_MODEL, IN_COLS), D_MODEL ** -0.5),
        "w_alpha_up": nrm(ks[2], (DEPTH, GATE_RANK, GLA_DK), GATE_RANK ** -0.5),
        "b_alpha": nrm(ks[3], (DEPTH, GLA_DK), 0.01),
        "gla_norm_g": 1.0 + nrm(ks[4], (DEPTH, GLA_HEADS, GLA_HEAD_DV), 0.02),
        "w_pool_grp": nrm(ks[5], (DEPTH, POOL_GROUPS, POOL_GROUP_DIM, POOL_GROUP_DIM), POOL_GROUP_DIM ** -0.5),
        "pool_scale": 1.0 + nrm(ks[6], (DEPTH, POOL_WIDTH), 0.02),
        "b_merge": nrm(ks[7], (DEPTH, N_BRANCHES * D_MODEL), 0.01),
        "w_proj_a": nrm(ks[8], (DEPTH, GLA_DV, D_MODEL), DEEPNORM_BETA * GLA_DV ** -0.5),
        "w_proj_b": nrm(ks[9], (DEPTH, POOL_WIDTH, D_MODEL), DEEPNORM_BETA * POOL_WIDTH ** -0.5),
        "w_out": nrm(ks[10], (DEPTH, D_MODEL, D_MODEL), DEEPNORM_BETA * D_MODEL ** -0.5),
        "ln_g": 1.0 + nrm(ks[11], (DEPTH, D_MODEL), 0.02),
        "ln_b": nrm(ks[12], (DEPTH, D_MODEL), 0.01),
    }


def reference(x, w_in, w_alpha_up, b_alpha, gla_norm_g, w_pool_grp, pool_scale,
              b_merge, w_proj_a, w_proj_b, w_out, ln_g, ln_b):
    for l in range(DEPTH):
        x = _layer(x, w_in[l], w_alpha_up[l], b_alpha[l], gla_norm_g[l], w_pool_grp[l],
                   pool_scale[l], b_merge[l], w_proj_a[l], w_proj_b[l], w_out[l],
                   ln_g[l], ln_b[l])
    return x
```

```python
import contextlib
import numpy as np
import concourse.bass as bass
import concourse.mybir as mybir
from concourse.bass_utils import run_bass_kernel_spmd

F32 = mybir.dt.float32
BF16 = mybir.dt.bfloat16
AF = mybir.ActivationFunctionType
ALU = mybir.AluOpType

D = 1024
INC = 7184
SEQ = 8192
BATCH = 4
DEPTH = 4
T = 256
NB = T // 128
NCH = T // 64
ALPHA = float((2.0 * DEPTH) ** 0.25)
EPS = 1e-5
QSCALE = float(128 ** -0.5)
NSLOT = 4
OPT = dict(cast_eng='act', ab_alias=True, ln_batch=False, ratio=2, defer=True, pb_first=True, add_eng='dve', dmat_act=False, knT_late=True, pair_defer=True)
WIN = (2, 4, 8, 16)

PIECE_ORDER = ['q', 'k', 'v0', 'v1', 'ga0', 'ga1', 'pl0', 'pl1', 'wp', 'gb0', 'gb1', 'ml0', 'ml1', 'ml2', 'ml3',
               'pa0', 'pa1', 'pb0', 'pb1', 'wo0', 'wo1']
WIN_OFF = {'q': 0, 'k': 512, 'v0': 1024, 'v1': 1536, 'ga0': 2048, 'ga1': 2560, 'pl0': 3088, 'pl1': 3600,
           'gb0': 4112, 'gb1': 4624, 'ml0': 5136, 'ml1': 5648, 'ml2': 6160, 'ml3': 6672}
AL_OFF = 3072
PID = {n: i for i, n in enumerate(PIECE_ORDER)}


class Prog:
    def __init__(self, nc):
        self.nc = nc
        self.ops = []
        self.last_w = {}
        self.readers = {}

    def op(self, eng, fn, reads=(), writes=(), dma_slot=None):
        idx = len(self.ops)
        deps = set()
        for r in reads:
            lw = self.last_w.get(r)
            if lw is not None:
                deps.add(lw)
            if isinstance(r, tuple) and r[0] == 'PS':
                last = {}
                for rd in self.readers.get(r, ()):
                    e_ = self.ops[rd]['eng']
                    if e_ != eng and rd > last.get(e_, -1):
                        last[e_] = rd
                deps.update(last.values())
        for w in writes:
            lw = self.last_w.get(w)
            if lw is not None:
                deps.add(lw)
            rl = self.readers.get(w)
            if rl:
                last = {}
                for rd in rl:
                    o_ = self.ops[rd]
                    if o_['dma'] is not None:
                        deps.add(rd)
                    elif rd > last.get(o_['eng'], -1):
                        last[o_['eng']] = rd
                deps.update(last.values())
        deps.discard(idx)
        self.ops.append(dict(eng=eng, fn=fn, deps=deps, dma=dma_slot, idx=idx, tag=getattr(self, 'tag', None), r=list(reads), w=list(writes)))
        for w in writes:
            self.last_w[w] = idx
            self.readers[w] = []
        ws = set(writes)
        for r in reads:
            if r not in ws:
                self.readers.setdefault(r, []).append(idx)
        return idx

    def emit(self, final_wait_ops=()):
        nc = self.nc
        ops = self.ops
        engs = ['pe', 'act', 'dve', 'pool', 'sp']

        def is_pe_pe(a, b):
            return a['eng'] == 'pe' and b['eng'] == 'pe' and a['dma'] is None and b['dma'] is None

        needed = set(final_wait_ops)
        for o in ops:
            for d in o['deps']:
                if not is_pe_pe(ops[d], o):
                    needed.add(d)
        tick = {e: 0 for e in engs}
        slot_cnt = {}
        for o in ops:
            if o['dma'] is not None:
                s = o['dma']
                slot_cnt[s] = slot_cnt.get(s, 0) + 16
                o['sem'] = ('dma', s)
                o['val'] = slot_cnt[s]
                o['inc'] = True
            else:
                if o['idx'] in needed:
                    tick[o['eng']] += 1
                    o['inc'] = True
                else:
                    o['inc'] = False
                o['sem'] = ('eng', o['eng'])
                o['val'] = tick[o['eng']]
        for o in ops:
            if o['dma'] is not None and str(o['dma']).startswith('G:'):
                o['val'] = slot_cnt[o['dma']]
        sem_names = sorted(set(o['sem'] for o in ops), key=str)
        self.n_sems = len(sem_names)
        self.nwaits = 0
        with contextlib.ExitStack() as st:
            sems = {}
            for i, sn in enumerate(sem_names):
                sems[sn] = st.enter_context(nc.semaphore("sem%d" % i))
            self.sem_map = {str(k): str(v) for k, v in sems.items()}
            block = st.enter_context(nc.Block())
            per_eng = {e: [o for o in ops if o['eng'] == e] for e in engs}

            def body(e, engine):
                known = {}
                for o in per_eng[e]:
                    req = {}
                    for d in o['deps']:
                        do = ops[d]
                        if is_pe_pe(do, o):
                            continue
                        k = do['sem']
                        if do['val'] > req.get(k, 0):
                            req[k] = do['val']
                    for k, v in req.items():
                        if known.get(k, 0) >= v:
                            continue
                        engine.wait_ge(sems[k], v)
                        self.nwaits += 1
                        known[k] = v
                    ins = o['fn'](engine)
                    try:
                        o['iname'] = ins.ins.name
                    except Exception:
                        o['iname'] = None
                    if o.get('tag') and getattr(self, 'annotate', False):
                        ins.annotate(o['tag'])
                    if o['inc']:
                        ins.then_inc(sems[o['sem']], 16 if o['dma'] is not None else 1)
                if e == 'sp':
                    fin = {}
                    for f in final_wait_ops:
                        fo = ops[f]
                        fin[fo['sem']] = max(fin.get(fo['sem'], 0), fo['val'])
                    for k, v in fin.items():
                        engine.wait_ge(sems[k], v)

            @block.tensor
            def _(eng):
                body('pe', eng)

            @block.scalar
            def _(eng):
                body('act', eng)

            @block.vector
            def _(eng):
                body('dve', eng)

            @block.gpsimd
            def _(eng):
                body('pool', eng)

            @block.sync
            def _(eng):
                body('sp', eng)


def make_consts():
    ident = np.eye(128, dtype=np.float32)
    s = np.arange(128)[:, None] % 64
    t = np.arange(64)[None, :]
    mA = (s <= t).astype(np.float32)
    mB = (s > t).astype(np.float32)
    m4 = np.zeros((128, 4, 2, 64), np.float32)
    m4[:, :, 0, :] = mA[:, None, :]
    m4[:, :, 1, :] = mB[:, None, :]
    rmask = np.ones((128, T), np.float32)
    rmask[:, ::64] = 0.0
    pm = np.zeros((128, 12, 128), np.float32)
    ss = np.arange(128)[:, None]
    tt = np.arange(128)[None, :]
    for g, w in enumerate(WIN):
        dcur = tt - ss
        pm[:, g, :] = ((dcur >= 0) & (dcur < w)) / float(w) - (dcur == 0)
        dprev = tt + 128 - ss
        pm[:, 4 + g, :] = ((dprev >= 0) & (dprev < w)) / float(w)
        cnt = np.minimum(tt + 1, w).astype(np.float32)
        pm[:, 8 + g, :] = ((dcur >= 0) & (dcur < w)) / cnt - (dcur == 0)
    return dict(c_ident=ident, c_mask=m4.reshape(128, 512), c_rmask=rmask, c_pm=pm.reshape(128, 12 * 128))


def build_nc(seq_len=SEQ, layers=(0, 1, 2, 3), interleave=True, annotate=False):
    nc = bass.Bass("TRN2", target_bir_lowering=False)
    NL = len(layers)
    NT = seq_len // T

    def din(name, shape):
        return nc.dram_tensor(name, list(shape), F32, kind="ExternalInput").ap()

    x_in = din("x", [seq_len, D])
    w_in = din("w_in", [DEPTH, D, INC])
    w_up = din("w_alpha_up", [DEPTH, 16, 512])
    b_al = din("b_alpha", [DEPTH, 512])
    gng_in = din("gla_norm_g", [DEPTH, 1024])
    wpg = din("w_pool_grp", [DEPTH, 4, 256, 256])
    psc_in = din("pool_scale", [DEPTH, 1024])
    bmg_in = din("b_merge", [DEPTH, 2048])
    w_pa = din("w_proj_a", [DEPTH, D, D])
    w_pb = din("w_proj_b", [DEPTH, D, D])
    w_wo = din("w_out", [DEPTH, D, D])
    lng_in = din("ln_g", [DEPTH, D])
    lnb_in = din("ln_b", [DEPTH, D])
    c_ident = din("c_ident", [128, 128])
    c_mask = din("c_mask", [128, 512])
    c_rmask = din("c_rmask", [128, T])
    c_pm = din("c_pm", [128, 12 * 128])
    out = nc.dram_tensor("out", [seq_len, D], F32, kind="ExternalOutput").ap()
    scr = [nc.dram_tensor("scr%d" % li, [len(PIECE_ORDER), 128, 4096], BF16).ap() for li in range(NL)]

    def SB(name, shape, dt):
        return nc.alloc_sbuf_tensor(name, list(shape), dt).ap()

    def PS(name, shape):
        return nc.alloc_psum_tensor(name, list(shape), F32).ap()

    xres = [[SB("xres%d_%d" % (p, b), [128, D], F32) for b in range(NB)] for p in range(2)]
    xT2 = [SB("xT%d" % p, [128, 8, T], BF16) for p in range(2)]
    xbf2 = [SB("xbf%d" % p, [128, NB, D], BF16) for p in range(2)]
    onT = SB("onT", [128, 2, 8, 128], BF16)
    W = [SB("W%d" % s, [128, 4096], BF16) for s in range(NSLOT)]
    al_sb = SB("al_sb", [128, NL, 8, 16], BF16)
    up_sb = SB("up_sb", [16, NL, 512], BF16)
    alT_sb = SB("alT_sb", [16, T], BF16)
    e_tmp = SB("e_tmp", [128, 4, T], F32)
    Pc = SB("Pc", [128, 4, T], F32)
    eG = SB("eG", [128, 4, T], F32)
    enG = SB("enG", [128, 4, T], F32)
    qg = SB("qg", [128, 4, T], BF16)
    qn = SB("qn", [128, 4, T], BF16)
    kn = SB("kn", [128, NB, 4, 128], BF16)
    kg = SB("kg", [128, 4, T], BF16)
    knT = SB("knT", [128, NB, 512], BF16)
    v_sb = SB("v_sb", [128, NB, 1024], BF16)
    sga = SB("sga", [128, 8, T], BF16)
    sgb = SB("sgb", [128, 8, T], BF16)
    uT = SB("uT", [128, 8, T], BF16)
    m_sb = SB("m_sb", [128, NB, 1024], BF16)
    mprev = SB("mprev", [128, NL, 1024], BF16)
    gate = SB("gate", [128, 16, T], BF16)
    tA = SB("tA", [128, 8, T], F32)
    tB = [SB("tB%d" % i, [128, 2, T], F32) for i in range(2)]
    mergedT = SB("mergedT", [128, 8, T], BF16)
    prod = SB("prod", [128, 4, 2, 64], F32)
    scT = SB("scT", [128, 4, 64], BF16)
    on = SB("on", [128, 1024], BF16)
    junk = SB("junk", [128, 256], BF16)
    t1 = SB("t1", [128, 4, 256], F32)
    S = SB("S", [128, NL, 4, 256], F32)
    Sbf = SB("Sbf", [128, NL, 4, 256], BF16)
    ss4 = SB("ss4", [128, 4], F32)
    ms4 = SB("ms4", [128, 4], F32)
    rstd4 = SB("rstd4", [128, 4], F32)
    lngb = SB("lngb", [128, 2, D], F32)
    bst = SB("bst", [128, 2, 6], F32)
    mv = SB("mv", [128, 2], F32)
    vpe = SB("vpe", [128, 1], F32)
    rstd1 = SB("rstd1", [128, 1], F32)
    bst2 = SB("bst2", [128, NB, 2, 6], F32)
    mv2 = SB("mv2", [128, NB, 2], F32)
    vpe2 = SB("vpe2", [128, NB], F32)
    rstd2 = SB("rstd2", [128, NB], F32)
    ident_f = SB("ident_f", [128, 128], F32)
    ident_b = SB("ident_b", [128, 128], BF16)
    mask4 = SB("mask4", [128, 4, 2, 64], F32)
    rmask = SB("rmask", [128, T], F32)
    pm = SB("pm", [128, 12, 128], BF16)
    nhalf = SB("nhalf", [128, 4], F32)
    one1 = SB("one1", [128, 1], F32)
    nb_al = SB("nb_al", [128, NL, 4], F32)
    gng = SB("gng", [128, NL, 8], F32)
    psc = SB("psc", [128, NL, 8], F32)
    bmg = SB("bmg", [128, NL, 16], F32)

    BANKS = {n: PS(n, [128, 512]) for n in ['P0', 'P1', 'P2', 'AB', 'O0', 'O1', 'U0', 'U1']}
    AB = BANKS['U0'] if OPT['ab_alias'] else BANKS['AB']
    ABKEY = ('PS', 'U0') if OPT['ab_alias'] else ('PS', 'AB')
    Obk = [BANKS['O0'], BANKS['O1']]
    Ubk = [BANKS['U0'], BANKS['U1']]
    pool_state = dict(names=['P0', 'P1', 'P2'], n=0)

    def set_pool(names):
        pool_state['names'] = list(names)

    def next_bank():
        nm = pool_state['names'][pool_state['n'] % len(pool_state['names'])]
        pool_state['n'] += 1
        return BANKS[nm], ('PS', nm)

    pg = Prog(nc)
    pg.annotate = annotate
    op = pg.op

    def MM(o_, l_, r_, st, sp):
        return lambda e: e.matmul(o_, lhsT=l_, rhs=r_, start=st, stop=sp)

    def TR(o_, i_, id_):
        return lambda e: e.transpose(o_, i_, id_)

    def ACT(o_, i_, f, bias=None, scale=None, accum=None):
        kw = {}
        if bias is not None:
            kw['bias'] = bias
        if scale is not None:
            kw['scale'] = scale
        if accum is not None:
            kw['accum_out'] = accum
        return lambda e: e.activation(out=o_, in_=i_, func=f, **kw)

    def TT(o_, a, b, o):
        return lambda e: e.tensor_tensor(out=o_, in0=a, in1=b, op=o)

    def TS(o_, a, s1, s2, o0, o1=None):
        if o1 is None:
            return lambda e: e.tensor_scalar(out=o_, in0=a, scalar1=s1, scalar2=s2, op0=o0)
        return lambda e: e.tensor_scalar(out=o_, in0=a, scalar1=s1, scalar2=s2, op0=o0, op1=o1)

    def STT(o_, a, s, b, o0, o1):
        return lambda e: e.scalar_tensor_tensor(out=o_, in0=a, scalar=s, in1=b, op0=o0, op1=o1)

    def DMA(o_, i_):
        return lambda e: e.dma_start(out=o_, in_=i_)

    def DMAs(o_, i_):
        return lambda e: e.dma_start(out=o_, in_=i_, allow_slow_non_contiguous=True)

    def CP(o_, i_):
        return lambda e: e.tensor_copy(out=o_, in_=i_)

    op('sp', DMA(ident_f, c_ident), writes=['ident_f'], dma_slot='G:c')
    op('sp', DMA(mask4, c_mask.rearrange("p (h a t) -> p h a t", h=4, a=2)), writes=['mask4'], dma_slot='G:c')
    op('sp', DMA(rmask, c_rmask), writes=['rmask'], dma_slot='G:c')
    for li, l in enumerate(layers):
        op('sp', DMAs(nb_al[:, li, :], b_al[l].rearrange("(h p) -> p h", p=128)), writes=[('nb_al', li)], dma_slot='G:c')
        op('sp', DMAs(gng[:, li, :], gng_in[l].rearrange("(c p) -> p c", p=128)), writes=[('gng', li)], dma_slot='G:c')
        op('sp', DMAs(psc[:, li, :], psc_in[l].rearrange("(c p) -> p c", p=128)), writes=[('psc', li)], dma_slot='G:c')
        op('sp', DMAs(bmg[:, li, :], bmg_in[l].rearrange("(c p) -> p c", p=128)), writes=[('bmg', li)], dma_slot='G:c')
    op('pool', DMA(pm, c_pm.rearrange("p (k t) -> p k t", k=12)), writes=['pm'], dma_slot='G:cp')
    for li, l in enumerate(layers):
        op('pool', DMA(al_sb[:, li, :, :], w_in[l][:, AL_OFF:AL_OFF + 16].rearrange("(kc p) c -> p kc c", p=128)),
           writes=[('al_sb', li)], dma_slot='G:cp')
        op('pool', DMA(up_sb[:, li, :], w_up[l]), writes=[('up_sb', li)], dma_slot='G:cp')
    for li, l in enumerate(layers):
        for name in PIECE_ORDER:
            pid = PID[name]
            grp = 'G:w%d_%d' % (li, 0 if pid < 8 else (1 if pid < 15 else 2))
            if name in WIN_OFF:
                c0 = WIN_OFF[name]
                src = w_in[l][:, c0:c0 + 512].rearrange("(kc p) c -> p kc c", p=128)
                dst = scr[li][pid].rearrange("p (kc c) -> p kc c", kc=8)
            elif name == 'wp':
                src = wpg[l].rearrange("g (kc p) o -> p g kc o", p=128)
                dst = scr[li][pid][:, 0:2048].rearrange("p (g kc o) -> p g kc o", g=4, kc=2)
            else:
                wsrc = {'pa': w_pa, 'pb': w_pb, 'wo': w_wo}[name[:2]]
                c0 = int(name[2]) * 512
                src = wsrc[l][:, c0:c0 + 512].rearrange("(kc p) c -> p kc c", p=128)
                dst = scr[li][pid].rearrange("p (kc c) -> p kc c", kc=8)
            op('pool', DMA(dst, src), writes=[('scr', li, name)], dma_slot=grp)
    op('dve', CP(ident_b, ident_f), reads=['ident_f'], writes=['ident_b'])
    op('dve', lambda e: e.memset(nhalf, -0.5), writes=['nhalf'])
    op('dve', lambda e: e.memset(one1, 1.0), writes=['one1'])
    op('dve', TS(nb_al, nb_al, -1.0, None, ALU.mult), reads=[('nb_al', li) for li in range(NL)], writes=[('nb_al', li) for li in range(NL)])
    op('pool', lambda e: e.memset(S, 0.0), writes=[('S', li) for li in range(NL)])
    op('pool', lambda e: e.memset(Sbf, 0.0), writes=[('Sbf', li) for li in range(NL)])

    porder = list(PIECE_ORDER)
    if OPT['pb_first']:
        ia, ib = porder.index('pa0'), porder.index('pb0')
        porder[ia:ia + 2], porder[ib:ib + 2] = ['pb0', 'pb1'], ['pa0', 'pa1']
    piece_seq = [(i, li, name) for m in range(NT // 2) for li in range(NL) for i in (2 * m, 2 * m + 1) for name in porder]
    state = dict(loaded=0, used=0)

    def ensure_loaded(upto):
        while state['loaded'] < min(upto, len(piece_seq)):
            k = state['loaded']
            i, li, name = piece_seq[k]
            s = k % NSLOT
            n = 2048 if name == 'wp' else 4096
            op('sp', DMA(W[s][:, 0:n], scr[li][PID[name]][:, 0:n]), reads=[('scr', li, name)], writes=[('W', s)],
               dma_slot='w%d' % s)
            state['loaded'] += 1

    def use_piece(i, li, name):
        k = state['used']
        assert piece_seq[k] == (i, li, name), (piece_seq[k], (i, li, name))
        ensure_loaded(k + NSLOT)
        state['used'] += 1
        return k % NSLOT

    def next_acc():
        bank, key = next_bank()
        return 0, bank[:, 0:T], key

    out_ops = []

    def fm_proj(s, c, rhs_fn, rhs_keys, nk=8):
        a, acc, akey = next_acc()
        Wv = W[s].rearrange("p (kc c) -> p kc c", kc=8)
        for kc in range(nk):
            op('pe', MM(acc, Wv[:, kc, c * 128:(c + 1) * 128], rhs_fn(kc), kc == 0, kc == nk - 1),
               reads=[('W', s)] + rhs_keys, writes=[akey])
        return acc, akey

    def DMAT(o_, i_):
        return lambda e: e.dma_start_transpose(out=o_, in_=i_)

    def fm_proj2(s, c0, rhs_fn, rhs_keys, nk=8):
        bank, key = next_bank()
        Wv = W[s].rearrange("p (kc c) -> p kc c", kc=8)
        for cc in range(2):
            c = c0 + cc
            for kc in range(nk):
                op('pe', MM(bank[:, cc * T:(cc + 1) * T], Wv[:, kc, c * 128:(c + 1) * 128], rhs_fn(kc), kc == 0, kc == nk - 1),
                   reads=[('W', s)] + rhs_keys, writes=[key])
        return bank, key

    def build_xT(par, b):
        xT, xbf = xT2[par], xbf2[par]
        if OPT['cast_eng'] == 'act':
            op('act', ACT(xbf[:, b, :], xres[par][b], AF.Copy), reads=[('xres', par, b)], writes=[('xbf', par, b)])
        else:
            op(OPT['cast_eng'], CP(xbf[:, b, :], xres[par][b]), reads=[('xres', par, b)], writes=[('xbf', par, b)])
        op('act' if (OPT['dmat_act'] and OPT['cast_eng'] == 'act') else 'sp',
           DMAT(xT[:, :, b * 128:(b + 1) * 128], xbf[:, b, :].rearrange("t (kc d) -> t kc d", kc=8)),
           reads=[('xbf', par, b)], writes=[('xT', par, b)], dma_slot='xt%d_%d' % (par, b))

    def phase_A(i, li, par):
        l = layers[li]
        xT = xT2[par]
        xT_keys = [('xT', par, b) for b in range(NB)]
        a, acc, akey = next_acc()
        for kc in range(8):
            op('pe', MM(acc[0:16, :], al_sb[:, li, kc, :], xT[:, kc, :], kc == 0, kc == 7),
               reads=[('al_sb', li)] + xT_keys, writes=[akey])
        op('act', ACT(alT_sb, acc[0:16, :], AF.Copy), reads=[akey], writes=['alT'])
        for hp in range(2):
            bank, bkey = next_bank()
            for hh in range(2):
                h = 2 * hp + hh
                op('pe', MM(bank[:, hh * T:(hh + 1) * T], up_sb[:, li, h * 128:(h + 1) * 128], alT_sb, True, True),
                   reads=[('up_sb', li), 'alT'], writes=[bkey])
            for hh in range(2):
                h = 2 * hp + hh
                op('act', ACT(e_tmp[:, h, :], bank[:, hh * T:(hh + 1) * T], AF.Exp, bias=nb_al[:, li, h:h + 1], scale=-1.0),
                   reads=[bkey, ('nb_al', li)], writes=[('e_tmp', h)])
        for h in range(4):
            op('act', ACT(e_tmp[:, h, :], e_tmp[:, h, :], AF.Ln, bias=one1, scale=1.0),
               reads=[('e_tmp', h), 'one1'], writes=[('e_tmp', h)])
        for h in range(4):
            op('dve', lambda e, h=h: e.tensor_tensor_scan(out=Pc[:, h, :], data0=rmask, data1=e_tmp[:, h, :], initial=0.0,
                                                         op0=ALU.mult, op1=ALU.add),
               reads=['rmask', ('e_tmp', h)], writes=[('Pc', h)])
        for h in range(4):
            op('act', ACT(eG[:, h, :], Pc[:, h, :], AF.Exp, scale=-1.0 / 16.0), reads=[('Pc', h)], writes=[('eG', h)])
            op('act', ACT(enG[:, h, :], Pc[:, h, :], AF.Exp, scale=1.0 / 16.0), reads=[('Pc', h)], writes=[('enG', h)])
        yield
        s = use_piece(i, li, 'q')
        for hp in range(2):
            hs = slice(2 * hp, 2 * hp + 2)
            bank, bkey = fm_proj2(s, 2 * hp, lambda kc: xT[:, kc, :], xT_keys)
            bv = bank.rearrange("p (h t) -> p h t", h=2)
            op('dve', STT(qg[:, hs, :], bv, QSCALE, eG[:, hs, :], ALU.mult, ALU.mult),
               reads=[bkey, ('eG', 2 * hp), ('eG', 2 * hp + 1)], writes=[('qg', 2 * hp), ('qg', 2 * hp + 1)])
            op('dve', STT(qn[:, hs, :], bv, QSCALE, enG[:, hs, :], ALU.mult, ALU.mult),
               reads=[bkey, ('enG', 2 * hp), ('enG', 2 * hp + 1)], writes=[('qn', 2 * hp), ('qn', 2 * hp + 1)])
        yield
        s = use_piece(i, li, 'k')
        for hp in range(2):
            hs = slice(2 * hp, 2 * hp + 2)
            bank, bkey = fm_proj2(s, 2 * hp, lambda kc: xT[:, kc, :], xT_keys)
            bv = bank.rearrange("p (h t) -> p h t", h=2)
            for hh in range(2):
                h = 2 * hp + hh
                op('dve', TT(kn[:, :, h, :], bank[:, hh * T:(hh + 1) * T].rearrange("p (b t) -> p b t", b=NB),
                             enG[:, h, :].rearrange("p (b t) -> p b t", b=NB), ALU.mult),
                   reads=[bkey, ('enG', h)], writes=[('kn', h)])
            op('dve', TT(kg[:, hs, :], bv, eG[:, hs, :], ALU.mult),
               reads=[bkey, ('eG', 2 * hp), ('eG', 2 * hp + 1)], writes=[('kg', 2 * hp), ('kg', 2 * hp + 1)])
        yield
        def knT_dmas():
            for b in range(NB):
                op('sp', DMAT(knT[:, b, :].rearrange("s (h d) -> s h d", h=4), kn[:, b, :, :]),
                   reads=[('kn', h) for h in range(4)], writes=[('knT', b)], dma_slot='knT%d' % b)
        if not OPT['knT_late']:
            knT_dmas()
        yield
        for j in range(2):
            s = use_piece(i, li, 'v%d' % j)
            Wv = W[s].rearrange("p (kc c) -> p kc c", kc=8)
            for b in range(NB):
                bank, bkey = next_bank()
                bkeys = [bkey]
                for kc in range(8):
                    op('pe', MM(bank, xT[:, kc, b * 128:(b + 1) * 128], Wv[:, kc, :], kc == 0, kc == 7),
                       reads=[('W', s), ('xT', par, b)], writes=bkeys)
                op('act' if (b + j) % 2 == 0 else 'dve',
                   ACT(v_sb[:, b, j * 512:(j + 1) * 512], bank, AF.Copy) if (b + j) % 2 == 0 else CP(v_sb[:, b, j * 512:(j + 1) * 512], bank),
                   reads=bkeys, writes=[('v', b, j)])
            yield
        for j in range(2):
            s = use_piece(i, li, 'ga%d' % j)
            for c0 in (0, 2):
                bank, bkey = fm_proj2(s, c0, lambda kc: xT[:, kc, :], xT_keys)
                vc = j * 4 + c0
                op('act', ACT(sga[:, vc:vc + 2, :], bank.rearrange("p (c t) -> p c t", c=2), AF.Silu),
                   reads=[bkey], writes=[('sga', vc), ('sga', vc + 1)])
            yield
        if OPT['knT_late']:
            knT_dmas()

    def phase_G(i, li, par):
        for b in range(NB):
            ABv = AB.rearrange("p (h a t) -> p h a t", h=4, a=2)
            for h in range(4):
                for hf in range(2):
                    c = 2 * b + hf
                    cs = slice(c * 64, (c + 1) * 64)
                    rs = slice(hf * 64, (hf + 1) * 64)
                    op('pe', MM(ABv[rs, h, 0, :], kn[:, b, h, hf * 64:(hf + 1) * 64], qg[:, h, cs], True, True),
                       reads=[('kn', h), ('qg', h)], writes=[ABKEY])
                    op('pe', MM(ABv[rs, h, 1, :], kg[:, h, cs], qn[:, h, cs], True, True),
                       reads=[('kg', h), ('qn', h)], writes=[ABKEY])
            op('dve', TT(prod, ABv, mask4, ALU.mult), reads=[ABKEY, 'mask4'], writes=['prod'])
            op('pool', TT(scT, prod[:, :, 0, :], prod[:, :, 1, :], ALU.add), reads=['prod'], writes=['scT'])
            yield
            for hf in range(2):
                c = 2 * b + hf
                cs = slice(c * 64, (c + 1) * 64)
                rs = slice(hf * 64, (hf + 1) * 64)
                for h in range(4):
                    o_out = Obk[h // 2][rs, (h % 2) * 256:(h % 2 + 1) * 256]
                    op('pe', MM(o_out, scT[rs, h, :], v_sb[rs, b, h * 256:(h + 1) * 256], True, False),
                       reads=['scT', ('v', b, h // 2)], writes=[('PS', 'O%d' % (h // 2))])
                    op('pe', MM(o_out, qg[:, h, cs], Sbf[:, li, h, :], False, True),
                       reads=[('qg', h), ('Sbf', li)], writes=[('PS', 'O%d' % (h // 2))])
                for h in range(4):
                    op('pe', MM(Ubk[h // 2][:, (h % 2) * 256:(h % 2 + 1) * 256], knT[rs, b, h * 128:(h + 1) * 128],
                                v_sb[rs, b, h * 256:(h + 1) * 256], True, True),
                       reads=[('knT', b), ('v', b, h // 2)], writes=[('PS', 'U%d' % (h // 2))])
                for hp in range(2):
                    op('dve', TT(t1[:, 2 * hp:2 * hp + 2, :], Ubk[hp].rearrange("p (h v) -> p h v", h=2),
                                 S[:, li, 2 * hp:2 * hp + 2, :], ALU.add),
                       reads=[('PS', 'U%d' % hp), ('S', li)], writes=[('t1', hp)])
                col = c * 64 + 63
                for h in range(4):
                    egl = eG[:, h, col:col + 1]
                    op('dve', TS(Sbf[:, li, h, :], t1[:, h, :], egl, None, ALU.mult),
                       reads=[('t1', h // 2), ('eG', h)], writes=[('Sbf', li)])
                    op('pool', TS(S[:, li, h, :], t1[:, h, :], egl, 0.0, ALU.mult, ALU.add),
                       reads=[('t1', h // 2), ('eG', h)], writes=[('S', li)])
                yield
            op('pool', lambda e: e.memset(ss4, 0.0), writes=['ss4'])
            for h in range(4):
                op('act', ACT(junk, Obk[h // 2][:, (h % 2) * 256:(h % 2 + 1) * 256], AF.Square, accum=ss4[:, h:h + 1]),
                   reads=[('PS', 'O%d' % (h // 2))], writes=['junk', 'ss4'])
            op('dve', TS(ms4, ss4, 1.0 / 256.0, EPS, ALU.mult, ALU.add), reads=['ss4'], writes=['ms4'])
            op('pool', TT(rstd4, ms4, nhalf, ALU.pow), reads=['ms4', 'nhalf'], writes=['rstd4'])
            for h in range(4):
                o_in = Obk[h // 2][:, (h % 2) * 256:(h % 2 + 1) * 256]
                if h < 2 or OPT['dmat_act']:
                    op('act', ACT(on[:, h * 256:(h + 1) * 256], o_in, AF.Identity, scale=rstd4[:, h:h + 1]),
                       reads=[('PS', 'O%d' % (h // 2)), 'rstd4'], writes=[('on', h)])
                else:
                    op('dve', TS(on[:, h * 256:(h + 1) * 256], o_in, rstd4[:, h:h + 1], None, ALU.mult),
                       reads=[('PS', 'O%d' % (h // 2)), 'rstd4'], writes=[('on', h)])
            yield
            op('act' if OPT['dmat_act'] else 'sp', DMAT(onT[:, b % 2, :, :], on.rearrange("t (vc v) -> t vc v", vc=8)),
               reads=[('on', h) for h in range(4)], writes=[('onT', b % 2)], dma_slot='onT%d' % (b % 2))
            for vc in range(8):
                dst = sga[:, vc, b * 128:(b + 1) * 128]
                op('dve', STT(dst, onT[:, b % 2, vc, :], gng[:, li, vc:vc + 1], dst, ALU.mult, ALU.mult),
                   reads=[('onT', b % 2), ('gng', li), ('sga', vc)], writes=[('sga', vc)])
            yield

    def phase_C(i, li, par):
        first_tile = (i == 0)
        xT = xT2[par]
        xT_keys = [('xT', par, b) for b in range(NB)]
        for j in range(2):
            s = use_piece(i, li, 'pl%d' % j)
            for c0 in (0, 2):
                bank, bkey = fm_proj2(s, c0, lambda kc: xT[:, kc, :], xT_keys)
                cc = j * 4 + c0
                bv = bank.rearrange("p (c t) -> p c t", c=2)
                if c0 == 0:
                    op('act', ACT(uT[:, cc:cc + 2, :], bv, AF.Copy), reads=[bkey], writes=[('uT', cc), ('uT', cc + 1)])
                else:
                    op('dve', CP(uT[:, cc:cc + 2, :], bv), reads=[bkey], writes=[('uT', cc), ('uT', cc + 1)])
                yield
        s = use_piece(i, li, 'wp')
        Wp = W[s][:, 0:2048].rearrange("p (g kc o) -> p g kc o", g=4, kc=2)
        for b in range(NB):
            for gp in range(2):
                bank, bkey = next_bank()
                bkeys = [bkey]
                for gg in range(2):
                    g = gp * 2 + gg
                    for kc in range(2):
                        op('pe', MM(bank[:, gg * 256:(gg + 1) * 256], uT[:, 2 * g + kc, b * 128:(b + 1) * 128],
                                    Wp[:, g, kc, :], kc == 0, kc == 1),
                           reads=[('W', s), ('uT', 2 * g + kc)], writes=bkeys)
                if gp == 0:
                    op('act', ACT(m_sb[:, b, gp * 512:(gp + 1) * 512], bank, AF.Copy), reads=bkeys, writes=[('m', b, gp)])
                else:
                    op('dve', CP(m_sb[:, b, gp * 512:(gp + 1) * 512], bank), reads=bkeys, writes=[('m', b, gp)])
            yield
        for j in range(2):
            s = use_piece(i, li, 'gb%d' % j)
            for c0 in (0, 2):
                oc0 = j * 4 + c0
                g = oc0 // 2
                bank, bkey = fm_proj2(s, c0, lambda kc: xT[:, kc, :], xT_keys)
                op('act', ACT(sgb[:, oc0:oc0 + 2, :], bank.rearrange("p (c t) -> p c t", c=2), AF.Silu),
                   reads=[bkey], writes=[('sgb', oc0), ('sgb', oc0 + 1)])
                bank2, bkey2 = next_bank()
                for cc in range(2):
                    oc = oc0 + cc
                    for b in range(NB):
                        first = first_tile and b == 0
                        o_sl = bank2[:, cc * T + b * 128:cc * T + (b + 1) * 128]
                        cur = m_sb[:, b, oc * 128:(oc + 1) * 128]
                        op('pe', MM(o_sl, cur, pm[:, (8 + g) if first else g, :], True, first),
                           reads=[('m', b, oc // 4), 'pm'], writes=[bkey2])
                        if not first:
                            if b > 0:
                                prv, pkey = m_sb[:, b - 1, oc * 128:(oc + 1) * 128], ('m', b - 1, oc // 4)
                            else:
                                prv, pkey = mprev[:, li, oc * 128:(oc + 1) * 128], ('mprev', li)
                            op('pe', MM(o_sl, prv, pm[:, 4 + g, :], False, True), reads=[pkey, 'pm'], writes=[bkey2])
                for cc in range(2):
                    oc = oc0 + cc
                    op('dve', STT(sgb[:, oc, :], bank2[:, cc * T:(cc + 1) * T], psc[:, li, oc:oc + 1], sgb[:, oc, :], ALU.mult, ALU.mult),
                       reads=[bkey2, ('psc', li), ('sgb', oc)], writes=[('sgb', oc)])
                yield
        op('pool', CP(mprev[:, li, :], m_sb[:, NB - 1, :]), reads=[('m', NB - 1, 0), ('m', NB - 1, 1)], writes=[('mprev', li)])
        for j in range(4):
            s = use_piece(i, li, 'ml%d' % j)
            for c0 in (0, 2):
                bank, bkey = fm_proj2(s, c0, lambda kc: xT[:, kc, :], xT_keys)
                for cc in range(2):
                    gi = j * 4 + c0 + cc
                    op('act', ACT(gate[:, gi, :], bank[:, cc * T:(cc + 1) * T], AF.Sigmoid, bias=bmg[:, li, gi:gi + 1], scale=1.0),
                       reads=[bkey, ('bmg', li)], writes=[('gate', gi)])
                yield

    def phase_D(i, li, par):
        l = layers[li]
        last_layer = (li == NL - 1)
        op('sp', DMA(lngb[:, 0, :], lng_in[l:l + 1, :].partition_broadcast(128)), writes=[('lngb', 0)], dma_slot='lngb0')
        op('sp', DMA(lngb[:, 1, :], lnb_in[l:l + 1, :].partition_broadcast(128)), writes=[('lngb', 1)], dma_slot='lngb1')
        sga_keys = [('sga', vc) for vc in range(8)]
        sgb_keys = [('sgb', vc) for vc in range(8)]
        first_nm, second_nm = ('pb', 'pa') if OPT['pb_first'] else ('pa', 'pb')
        src = {'pa': (sga, sga_keys, 0), 'pb': (sgb, sgb_keys, 8)}
        f_buf, f_keys, f_g = src[first_nm]
        s_buf, s_keys, s_g = src[second_nm]
        for j in range(2):
            s = use_piece(i, li, '%s%d' % (first_nm, j))
            for c0 in (0, 2):
                dc = j * 4 + c0
                bank, bkey = fm_proj2(s, c0, lambda kc: f_buf[:, kc, :], f_keys)
                op('dve', TT(tA[:, dc:dc + 2, :], bank.rearrange("p (c t) -> p c t", c=2), gate[:, f_g + dc:f_g + dc + 2, :], ALU.mult),
                   reads=[bkey, ('gate', f_g + dc), ('gate', f_g + dc + 1)], writes=[('tA', dc), ('tA', dc + 1)])
            yield
        for j in range(2):
            s = use_piece(i, li, '%s%d' % (second_nm, j))
            for c0 in (0, 2):
                dc = j * 4 + c0
                bank, bkey = fm_proj2(s, c0, lambda kc: s_buf[:, kc, :], s_keys)
                tb = tB[(dc // 2) % 2]
                tkey = ('tB', (dc // 2) % 2)
                op('dve', TT(tb, bank.rearrange("p (c t) -> p c t", c=2), gate[:, s_g + dc:s_g + dc + 2, :], ALU.mult),
                   reads=[bkey, ('gate', s_g + dc), ('gate', s_g + dc + 1)], writes=[tkey])
                op(OPT['add_eng'], TT(mergedT[:, dc:dc + 2, :], tb, tA[:, dc:dc + 2, :], ALU.add), reads=[tkey, ('tA', dc), ('tA', dc + 1)],
                   writes=[('mg', dc), ('mg', dc + 1)])
            yield
        mg_keys = [('mg', dc) for dc in range(8)]
        Ybanks = [Obk, Ubk]
        Ykeys = [[('PS', 'O0'), ('PS', 'O1')], [('PS', 'U0'), ('PS', 'U1')]]
        for j in range(2):
            s = use_piece(i, li, 'wo%d' % j)
            Wv = W[s].rearrange("p (kc c) -> p kc c", kc=8)
            for b in range(NB):
                for dc in range(8):
                    op('pe', MM(Ybanks[b % 2][j], mergedT[:, dc, b * 128:(b + 1) * 128], Wv[:, dc, :], dc == 0, dc == 7),
                       reads=[('W', s)] + mg_keys, writes=[Ykeys[b % 2][j]])
            yield
        if not OPT['ln_batch']:
            for b in range(NB):
                Yb = Ybanks[b % 2]
                ykeys = Ykeys[b % 2]
                xr = xres[par][b]
                xk = ('xres', par, b)
                for j in range(2):
                    xh = xr[:, j * 512:(j + 1) * 512]
                    op('dve', STT(xh, xh, ALPHA, Yb[j], ALU.mult, ALU.add), reads=[xk, ykeys[j]], writes=[xk])
                for j in range(2):
                    op('dve', lambda e, j=j, xr=xr: e.bn_stats(out=bst[:, j, :], in_=xr[:, j * 512:(j + 1) * 512]),
                       reads=[xk], writes=['bst'])
                op('dve', lambda e: e.bn_aggr(out=mv, in_=bst.rearrange("p a s -> p (a s)")), reads=['bst'], writes=['mv'])
                op('dve', TS(vpe, mv[:, 1:2], EPS, None, ALU.add), reads=['mv'], writes=['vpe'])
                op('pool', TT(rstd1, vpe, nhalf[:, 0:1], ALU.pow), reads=['vpe', 'nhalf'], writes=['rstd1'])
                op('dve', TS(xr, xr, mv[:, 0:1], rstd1, ALU.subtract, ALU.mult), reads=[xk, 'mv', 'rstd1'], writes=[xk])
                op('pool', TT(xr, xr, lngb[:, 0, :], ALU.mult), reads=[xk, ('lngb', 0)], writes=[xk])
                op('pool', TT(xr, xr, lngb[:, 1, :], ALU.add), reads=[xk, ('lngb', 1)], writes=[xk])
                if last_layer:
                    r0 = i * T + b * 128
                    out_ops.append(op('sp', DMA(out[r0:r0 + 128, :], xr), reads=[xk], dma_slot='out%d_%d' % (par, b)))
                elif OPT['defer']:
                    deferred.append((par, b))
                else:
                    build_xT(par, b)
                yield
        else:
            for b in range(NB):
                Yb = Ybanks[b % 2]
                ykeys = Ykeys[b % 2]
                xr = xres[par][b]
                xk = ('xres', par, b)
                for j in range(2):
                    xh = xr[:, j * 512:(j + 1) * 512]
                    op('dve', STT(xh, xh, ALPHA, Yb[j], ALU.mult, ALU.add), reads=[xk, ykeys[j]], writes=[xk])
                for j in range(2):
                    op('dve', lambda e, j=j, xr=xr, b=b: e.bn_stats(out=bst2[:, b, j, :], in_=xr[:, j * 512:(j + 1) * 512]),
                       reads=[xk], writes=[('bst', b)])
                op('dve', lambda e, b=b: e.bn_aggr(out=mv2[:, b, :], in_=bst2[:, b, :, :].rearrange("p a s -> p (a s)")),
                   reads=[('bst', b)], writes=[('mv', b)])
            op('dve', TS(vpe2, mv2[:, :, 1], EPS, None, ALU.add), reads=[('mv', b) for b in range(NB)], writes=['vpe2'])
            op('pool', TT(rstd2, vpe2, nhalf[:, 0:NB], ALU.pow), reads=['vpe2', 'nhalf'], writes=['rstd2'])
            yield
            for b in range(NB):
                xr = xres[par][b]
                xk = ('xres', par, b)
                op('dve', TS(xr, xr, mv2[:, b, 0:1], rstd2[:, b:b + 1], ALU.subtract, ALU.mult), reads=[xk, ('mv', b), 'rstd2'], writes=[xk])
            for b in range(NB):
                xr = xres[par][b]
                xk = ('xres', par, b)
                op('pool', TT(xr, xr, lngb[:, 0, :], ALU.mult), reads=[xk, ('lngb', 0)], writes=[xk])
                op('pool', TT(xr, xr, lngb[:, 1, :], ALU.add), reads=[xk, ('lngb', 1)], writes=[xk])
                if last_layer:
                    r0 = i * T + b * 128
                    out_ops.append(op('sp', DMA(out[r0:r0 + 128, :], xr), reads=[xk], dma_slot='out%d_%d' % (par, b)))
                else:
                    build_xT(par, b)
            yield

    def load_x(i):
        par = i % 2
        for b in range(NB):
            r0 = i * T + b * 128
            op('sp', DMA(xres[par][b], x_in[r0:r0 + 128, :]), writes=[('xres', par, b)], dma_slot='xin%d_%d' % (par, b))

    def run_all(*gens):
        for g in gens:
            for _ in g:
                pass

    def run_interleaved(ga, gb, ratio=None):
        ratio = ratio or OPT['ratio']
        done_a = done_b = False
        while not (done_a and done_b):
            if not done_a:
                try:
                    next(ga)
                except StopIteration:
                    done_a = True
            for _ in range(ratio):
                if not done_b:
                    try:
                        next(gb)
                    except StopIteration:
                        done_b = True

    deferred = []

    def flush_deferred():
        pend = list(deferred)
        del deferred[:]
        for (p_, b_) in pend:
            build_xT(p_, b_)

    def tile_layer(i, li, par):
        set_pool(['P0', 'P1', 'P2', 'AB', 'O0', 'O1', 'U0', 'U1'])
        pg.tag = 'PH_A_%d_%d' % (i, li)
        if [d for d in deferred if d[0] == par]:
            flush_deferred()
        run_all(phase_A(i, li, par))
        flush_deferred()
        pg.tag = 'PH_GC_%d_%d' % (i, li)
        set_pool(['P0', 'P1', 'P2', 'AB'] if OPT['ab_alias'] else ['P0', 'P1', 'P2'])
        if interleave:
            run_interleaved(phase_G(i, li, par), phase_C(i, li, par))
        else:
            run_all(phase_G(i, li, par), phase_C(i, li, par))
        set_pool(['P0', 'P1', 'P2', 'AB'])
        pg.tag = 'PH_D_%d_%d' % (i, li)
        run_all(phase_D(i, li, par))

    assert NT % 2 == 0
    set_pool(['P0', 'P1', 'P2', 'AB', 'O0', 'O1', 'U0', 'U1'])
    for m in range(NT // 2):
        tiles = (2 * m, 2 * m + 1)
        for par, i in enumerate(tiles):
            if m == 0:
                load_x(i)
            if m == 0 or not OPT['pair_defer']:
                for b in range(NB):
                    build_xT(par, b)
        for li in range(NL):
            for par, i in enumerate(tiles):
                tile_layer(i, li, par)
                if li == NL - 1 and m + 1 < NT // 2:
                    load_x(i + 2)
                    if OPT['pair_defer']:
                        deferred.extend((par, b) for b in range(NB))

    pg.emit(final_wait_ops=out_ops)
    return nc, pg


PARAM_NAMES = ["w_in", "w_alpha_up", "b_alpha", "gla_norm_g", "w_pool_grp", "pool_scale", "b_merge",
               "w_proj_a", "w_proj_b", "w_out", "ln_g", "ln_b"]


def _prep_params(inputs):
    p = {}
    for k in PARAM_NAMES:
        a = np.ascontiguousarray(np.asarray(inputs[k], dtype=np.float32))
        if k == "gla_norm_g":
            a = a.reshape(DEPTH, 1024)
        p[k] = a
    p.update(make_consts())
    return p


_NC_CACHE = {}


def kernel(**inputs):
    x = np.ascontiguousarray(np.asarray(inputs["x"], dtype=np.float32))
    params = _prep_params(inputs)
    key = (x.shape[1], (0, 1, 2, 3))
    if key not in _NC_CACHE:
        _NC_CACHE[key] = build_nc(x.shape[1], (0, 1, 2, 3))[0]
    nc = _NC_CACHE[key]
    n = 8
    consts = make_consts()
    zero_map = {k: np.zeros_like(v) for k, v in params.items() if k not in consts}
    zero_map.update(consts)
    zero_map["x"] = np.zeros_like(x[0])
    in_maps = []
    for c in range(n):
        if c % 2 == 0:
            m = dict(params)
            m["x"] = x[c // 2]
        else:
            m = zero_map
        in_maps.append(m)
    res = run_bass_kernel_spmd(nc, in_maps, core_ids=list(range(n)))
    return np.stack([res.results[2 * b]["out"] for b in range(BATCH)], axis=0).astype(np.float32)
```

```python
import contextlib
import numpy as np
import concourse.bass as bass
import concourse.mybir as mybir
from concourse.bass_utils import run_bass_kernel_spmd

F32 = mybir.dt.float32
BF16 = mybir.dt.bfloat16
AF = mybir.ActivationFunctionType
ALU = mybir.AluOpType

D = 1024
INC = 7184
SEQ = 8192
BATCH = 4
DEPTH = 4
T = 256
NB = T // 128
NCH = T // 64
ALPHA = float((2.0 * DEPTH) ** 0.25)
EPS = 1e-5
QSCALE = float(128 ** -0.5)
NSLOT = 4
OPT = dict(cast_eng='act', ab_alias=True, ln_batch=False, ratio=2, defer=True, pb_first=True, add_eng='dve', dmat_act=False, knT_late=True, pair_defer=True, norm_yield=0, on_eng='split', mid_yield=0, copy_act=False, g_pre=False, g_pipe=True)
WIN = (2, 4, 8, 16)

PIECE_ORDER = ['q', 'k', 'v0', 'v1', 'ga0', 'ga1', 'pl0', 'pl1', 'wp', 'gb0', 'gb1', 'ml0', 'ml1', 'ml2', 'ml3',
               'pa0', 'pa1', 'pb0', 'pb1', 'wo0', 'wo1']
WIN_OFF = {'q': 0, 'k': 512, 'v0': 1024, 'v1': 1536, 'ga0': 2048, 'ga1': 2560, 'pl0': 3088, 'pl1': 3600,
           'gb0': 4112, 'gb1': 4624, 'ml0': 5136, 'ml1': 5648, 'ml2': 6160, 'ml3': 6672}
AL_OFF = 3072
PID = {n: i for i, n in enumerate(PIECE_ORDER)}


class Prog:
    def __init__(self, nc):
        self.nc = nc
        self.ops = []
        self.last_w = {}
        self.readers = {}

    def op(self, eng, fn, reads=(), writes=(), dma_slot=None):
        idx = len(self.ops)
        deps = set()
        for r in reads:
            lw = self.last_w.get(r)
            if lw is not None:
                deps.add(lw)
            if isinstance(r, tuple) and r[0] == 'PS':
                last = {}
                for rd in self.readers.get(r, ()):
                    e_ = self.ops[rd]['eng']
                    if e_ != eng and rd > last.get(e_, -1):
                        last[e_] = rd
                deps.update(last.values())
        for w in writes:
            lw = self.last_w.get(w)
            if lw is not None:
                deps.add(lw)
            rl = self.readers.get(w)
            if rl:
                last = {}
                for rd in rl:
                    o_ = self.ops[rd]
                    if o_['dma'] is not None:
                        deps.add(rd)
                    elif rd > last.get(o_['eng'], -1):
                        last[o_['eng']] = rd
                deps.update(last.values())
        deps.discard(idx)
        self.ops.append(dict(eng=eng, fn=fn, deps=deps, dma=dma_slot, idx=idx, tag=getattr(self, 'tag', None), r=list(reads), w=list(writes)))
        for w in writes:
            self.last_w[w] = idx
            self.readers[w] = []
        ws = set(writes)
        for r in reads:
            if r not in ws:
                self.readers.setdefault(r, []).append(idx)
        return idx

    def emit(self, final_wait_ops=()):
        nc = self.nc
        ops = self.ops
        engs = ['pe', 'act', 'dve', 'pool', 'sp']

        def is_pe_pe(a, b):
            return a['eng'] == 'pe' and b['eng'] == 'pe' and a['dma'] is None and b['dma'] is None

        needed = set(final_wait_ops)
        for o in ops:
            for d in o['deps']:
                if not is_pe_pe(ops[d], o):
                    needed.add(d)
        tick = {e: 0 for e in engs}
        slot_cnt = {}
        for o in ops:
            if o['dma'] is not None:
                s = o['dma']
                slot_cnt[s] = slot_cnt.get(s, 0) + 16
                o['sem'] = ('dma', s)
                o['val'] = slot_cnt[s]
                o['inc'] = True
            else:
                if o['idx'] in needed:
                    tick[o['eng']] += 1
                    o['inc'] = True
                else:
                    o['inc'] = False
                o['sem'] = ('eng', o['eng'])
                o['val'] = tick[o['eng']]
        for o in ops:
            if o['dma'] is not None and str(o['dma']).startswith('G:'):
                o['val'] = slot_cnt[o['dma']]
        sem_names = sorted(set(o['sem'] for o in ops), key=str)
        self.n_sems = len(sem_names)
        self.nwaits = 0
        with contextlib.ExitStack() as st:
            sems = {}
            for i, sn in enumerate(sem_names):
                sems[sn] = st.enter_context(nc.semaphore("sem%d" % i))
            self.sem_map = {str(k): str(v) for k, v in sems.items()}
            block = st.enter_context(nc.Block())
            per_eng = {e: [o for o in ops if o['eng'] == e] for e in engs}

            def body(e, engine):
                known = {}
                for o in per_eng[e]:
                    req = {}
                    for d in o['deps']:
                        do = ops[d]
                        if is_pe_pe(do, o):
                            continue
                        k = do['sem']
                        if do['val'] > req.get(k, 0):
                            req[k] = do['val']
                    for k, v in req.items():
                        if known.get(k, 0) >= v:
                            continue
                        engine.wait_ge(sems[k], v)
                        self.nwaits += 1
                        known[k] = v
                    ins = o['fn'](engine)
                    try:
                        o['iname'] = ins.ins.name
                    except Exception:
                        o['iname'] = None
                    if o.get('tag') and getattr(self, 'annotate', False):
                        ins.annotate(o['tag'])
                    if o['inc']:
                        ins.then_inc(sems[o['sem']], 16 if o['dma'] is not None else 1)
                if e == 'sp':
                    fin = {}
                    for f in final_wait_ops:
                        fo = ops[f]
                        fin[fo['sem']] = max(fin.get(fo['sem'], 0), fo['val'])
                    for k, v in fin.items():
                        engine.wait_ge(sems[k], v)

            @block.tensor
            def _(eng):
                body('pe', eng)

            @block.scalar
            def _(eng):
                body('act', eng)

            @block.vector
            def _(eng):
                body('dve', eng)

            @block.gpsimd
            def _(eng):
                body('pool', eng)

            @block.sync
            def _(eng):
                body('sp', eng)


def make_consts():
    ident = np.eye(128, dtype=np.float32)
    s = np.arange(128)[:, None] % 64
    t = np.arange(64)[None, :]
    mA = (s <= t).astype(np.float32)
    mB = (s > t).astype(np.float32)
    m4 = np.zeros((128, 4, 2, 64), np.float32)
    m4[:, :, 0, :] = mA[:, None, :]
    m4[:, :, 1, :] = mB[:, None, :]
    rmask = np.ones((128, T), np.float32)
    rmask[:, ::64] = 0.0
    pm = np.zeros((128, 12, 128), np.float32)
    ss = np.arange(128)[:, None]
    tt = np.arange(128)[None, :]
    for g, w in enumerate(WIN):
        dcur = tt - ss
        pm[:, g, :] = ((dcur >= 0) & (dcur < w)) / float(w) - (dcur == 0)
        dprev = tt + 128 - ss
        pm[:, 4 + g, :] = ((dprev >= 0) & (dprev < w)) / float(w)
        cnt = np.minimum(tt + 1, w).astype(np.float32)
        pm[:, 8 + g, :] = ((dcur >= 0) & (dcur < w)) / cnt - (dcur == 0)
    return dict(c_ident=ident, c_mask=m4.reshape(128, 512), c_rmask=rmask, c_pm=pm.reshape(128, 12 * 128))


def build_nc(seq_len=SEQ, layers=(0, 1, 2, 3), interleave=True, annotate=False):
    nc = bass.Bass("TRN2", target_bir_lowering=False)
    NL = len(layers)
    NT = seq_len // T

    def din(name, shape):
        return nc.dram_tensor(name, list(shape), F32, kind="ExternalInput").ap()

    x_in = din("x", [seq_len, D])
    w_in = din("w_in", [DEPTH, D, INC])
    w_up = din("w_alpha_up", [DEPTH, 16, 512])
    b_al = din("b_alpha", [DEPTH, 512])
    gng_in = din("gla_norm_g", [DEPTH, 1024])
    wpg = din("w_pool_grp", [DEPTH, 4, 256, 256])
    psc_in = din("pool_scale", [DEPTH, 1024])
    bmg_in = din("b_merge", [DEPTH, 2048])
    w_pa = din("w_proj_a", [DEPTH, D, D])
    w_pb = din("w_proj_b", [DEPTH, D, D])
    w_wo = din("w_out", [DEPTH, D, D])
    lng_in = din("ln_g", [DEPTH, D])
    lnb_in = din("ln_b", [DEPTH, D])
    c_ident = din("c_ident", [128, 128])
    c_mask = din("c_mask", [128, 512])
    c_rmask = din("c_rmask", [128, T])
    c_pm = din("c_pm", [128, 12 * 128])
    out = nc.dram_tensor("out", [seq_len, D], F32, kind="ExternalOutput").ap()
    scr = [nc.dram_tensor("scr%d" % li, [len(PIECE_ORDER), 128, 4096], BF16).ap() for li in range(NL)]

    def SB(name, shape, dt):
        return nc.alloc_sbuf_tensor(name, list(shape), dt).ap()

    def PS(name, shape):
        return nc.alloc_psum_tensor(name, list(shape), F32).ap()

    xres = [[SB("xres%d_%d" % (p, b), [128, D], F32) for b in range(NB)] for p in range(2)]
    xT2 = [SB("xT%d" % p, [128, 8, T], BF16) for p in range(2)]
    xbf2 = [SB("xbf%d" % p, [128, NB, D], BF16) for p in range(2)]
    onT = SB("onT", [128, 2, 8, 128], BF16)
    W = [SB("W%d" % s, [128, 4096], BF16) for s in range(NSLOT)]
    al_sb = SB("al_sb", [128, NL, 8, 16], BF16)
    up_sb = SB("up_sb", [16, NL, 512], BF16)
    alT_sb = SB("alT_sb", [16, T], BF16)
    e_tmp = SB("e_tmp", [128, 4, T], F32)
    Pc = SB("Pc", [128, 4, T], F32)
    eG = SB("eG", [128, 4, T], F32)
    enG = SB("enG", [128, 4, T], F32)
    qg = SB("qg", [128, 4, T], BF16)
    qn = SB("qn", [128, 4, T], BF16)
    kn = SB("kn", [128, NB, 4, 128], BF16)
    kg = SB("kg", [128, 4, T], BF16)
    knT = SB("knT", [128, NB, 512], BF16)
    v_sb = SB("v_sb", [128, NB, 1024], BF16)
    sga = SB("sga", [128, 8, T], BF16)
    sgb = SB("sgb", [128, 8, T], BF16)
    uT = SB("uT", [128, 8, T], BF16)
    m_sb = SB("m_sb", [128, NB, 1024], BF16)
    mprev = SB("mprev", [128, NL, 1024], BF16)
    gate = SB("gate", [128, 16, T], BF16)
    tA = SB("tA", [128, 8, T], F32)
    tB = [SB("tB%d" % i, [128, 2, T], F32) for i in range(2)]
    mergedT = SB("mergedT", [128, 8, T], BF16)
    prod = SB("prod", [128, 4, 2, 64], F32)
    scT = SB("scT", [128, 4, 64], BF16)
    on = SB("on", [128, 1024], BF16)
    junk = SB("junk", [128, 256], BF16)
    t1 = SB("t1", [128, 4, 256], F32)
    S = SB("S", [128, NL, 4, 256], F32)
    Sbf = SB("Sbf", [128, NL, 4, 256], BF16)
    ss4 = SB("ss4", [128, 4], F32)
    ms4 = SB("ms4", [128, 4], F32)
    rstd4 = SB("rstd4", [128, 4], F32)
    lngb = SB("lngb", [128, 2, D], F32)
    bst = SB("bst", [128, 2, 6], F32)
    mv = SB("mv", [128, 2], F32)
    vpe = SB("vpe", [128, 1], F32)
    rstd1 = SB("rstd1", [128, 1], F32)
    bst2 = SB("bst2", [128, NB, 2, 6], F32)
    mv2 = SB("mv2", [128, NB, 2], F32)
    vpe2 = SB("vpe2", [128, NB], F32)
    rstd2 = SB("rstd2", [128, NB], F32)
    ident_f = SB("ident_f", [128, 128], F32)
    ident_b = SB("ident_b", [128, 128], BF16)
    mask4 = SB("mask4", [128, 4, 2, 64], F32)
    rmask = SB("rmask", [128, T], F32)
    pm = SB("pm", [128, 12, 128], BF16)
    nhalf = SB("nhalf", [128, 4], F32)
    one1 = SB("one1", [128, 1], F32)
    nb_al = SB("nb_al", [128, NL, 4], F32)
    gng = SB("gng", [128, NL, 8], F32)
    psc = SB("psc", [128, NL, 8], F32)
    bmg = SB("bmg", [128, NL, 16], F32)

    BANKS = {n: PS(n, [128, 512]) for n in ['P0', 'P1', 'P2', 'AB', 'O0', 'O1', 'U0', 'U1']}
    AB = BANKS['U0'] if OPT['ab_alias'] else BANKS['AB']
    ABKEY = ('PS', 'U0') if OPT['ab_alias'] else ('PS', 'AB')
    Obk = [BANKS['O0'], BANKS['O1']]
    Ubk = [BANKS['U0'], BANKS['U1']]
    pool_state = dict(names=['P0', 'P1', 'P2'], n=0)

    def set_pool(names):
        pool_state['names'] = list(names)

    def next_bank():
        nm = pool_state['names'][pool_state['n'] % len(pool_state['names'])]
        pool_state['n'] += 1
        return BANKS[nm], ('PS', nm)

    pg = Prog(nc)
    pg.annotate = annotate
    op = pg.op

    def MM(o_, l_, r_, st, sp):
        return lambda e: e.matmul(o_, lhsT=l_, rhs=r_, start=st, stop=sp)

    def TR(o_, i_, id_):
        return lambda e: e.transpose(o_, i_, id_)

    def ACT(o_, i_, f, bias=None, scale=None, accum=None):
        kw = {}
        if bias is not None:
            kw['bias'] = bias
        if scale is not None:
            kw['scale'] = scale
        if accum is not None:
            kw['accum_out'] = accum
        return lambda e: e.activation(out=o_, in_=i_, func=f, **kw)

    def TT(o_, a, b, o):
        return lambda e: e.tensor_tensor(out=o_, in0=a, in1=b, op=o)

    def TS(o_, a, s1, s2, o0, o1=None):
        if o1 is None:
            return lambda e: e.tensor_scalar(out=o_, in0=a, scalar1=s1, scalar2=s2, op0=o0)
        return lambda e: e.tensor_scalar(out=o_, in0=a, scalar1=s1, scalar2=s2, op0=o0, op1=o1)

    def STT(o_, a, s, b, o0, o1):
        return lambda e: e.scalar_tensor_tensor(out=o_, in0=a, scalar=s, in1=b, op0=o0, op1=o1)

    def DMA(o_, i_):
        return lambda e: e.dma_start(out=o_, in_=i_)

    def DMAs(o_, i_):
        return lambda e: e.dma_start(out=o_, in_=i_, allow_slow_non_contiguous=True)

    def CP(o_, i_):
        return lambda e: e.tensor_copy(out=o_, in_=i_)

    op('sp', DMA(ident_f, c_ident), writes=['ident_f'], dma_slot='G:c')
    op('sp', DMA(mask4, c_mask.rearrange("p (h a t) -> p h a t", h=4, a=2)), writes=['mask4'], dma_slot='G:c')
    op('sp', DMA(rmask, c_rmask), writes=['rmask'], dma_slot='G:c')
    for li, l in enumerate(layers):
        op('sp', DMAs(nb_al[:, li, :], b_al[l].rearrange("(h p) -> p h", p=128)), writes=[('nb_al', li)], dma_slot='G:c')
        op('sp', DMAs(gng[:, li, :], gng_in[l].rearrange("(c p) -> p c", p=128)), writes=[('gng', li)], dma_slot='G:c')
        op('sp', DMAs(psc[:, li, :], psc_in[l].rearrange("(c p) -> p c", p=128)), writes=[('psc', li)], dma_slot='G:c')
        op('sp', DMAs(bmg[:, li, :], bmg_in[l].rearrange("(c p) -> p c", p=128)), writes=[('bmg', li)], dma_slot='G:c')
    op('pool', DMA(pm, c_pm.rearrange("p (k t) -> p k t", k=12)), writes=['pm'], dma_slot='G:cp')
    for li, l in enumerate(layers):
        op('pool', DMA(al_sb[:, li, :, :], w_in[l][:, AL_OFF:AL_OFF + 16].rearrange("(kc p) c -> p kc c", p=128)),
           writes=[('al_sb', li)], dma_slot='G:cp')
        op('pool', DMA(up_sb[:, li, :], w_up[l]), writes=[('up_sb', li)], dma_slot='G:cp')
    for li, l in enumerate(layers):
        for name in PIECE_ORDER:
            pid = PID[name]
            grp = 'G:w%d_%d' % (li, 0 if pid < 8 else (1 if pid < 15 else 2))
            if name in WIN_OFF:
                c0 = WIN_OFF[name]
                src = w_in[l][:, c0:c0 + 512].rearrange("(kc p) c -> p kc c", p=128)
                dst = scr[li][pid].rearrange("p (kc c) -> p kc c", kc=8)
            elif name == 'wp':
                src = wpg[l].rearrange("g (kc p) o -> p g kc o", p=128)
                dst = scr[li][pid][:, 0:2048].rearrange("p (g kc o) -> p g kc o", g=4, kc=2)
            else:
                wsrc = {'pa': w_pa, 'pb': w_pb, 'wo': w_wo}[name[:2]]
                c0 = int(name[2]) * 512
                src = wsrc[l][:, c0:c0 + 512].rearrange("(kc p) c -> p kc c", p=128)
                dst = scr[li][pid].rearrange("p (kc c) -> p kc c", kc=8)
            op('pool', DMA(dst, src), writes=[('scr', li, name)], dma_slot=grp)
    op('dve', CP(ident_b, ident_f), reads=['ident_f'], writes=['ident_b'])
    op('dve', lambda e: e.memset(nhalf, -0.5), writes=['nhalf'])
    op('dve', lambda e: e.memset(one1, 1.0), writes=['one1'])
    op('dve', TS(nb_al, nb_al, -1.0, None, ALU.mult), reads=[('nb_al', li) for li in range(NL)], writes=[('nb_al', li) for li in range(NL)])
    op('pool', lambda e: e.memset(S, 0.0), writes=[('S', li) for li in range(NL)])
    op('pool', lambda e: e.memset(Sbf, 0.0), writes=[('Sbf', li) for li in range(NL)])

    porder = list(PIECE_ORDER)
    if OPT['pb_first']:
        ia, ib = porder.index('pa0'), porder.index('pb0')
        porder[ia:ia + 2], porder[ib:ib + 2] = ['pb0', 'pb1'], ['pa0', 'pa1']
    piece_seq = [(i, li, name) for m in range(NT // 2) for li in range(NL) for i in (2 * m, 2 * m + 1) for name in porder]
    state = dict(loaded=0, used=0)

    def ensure_loaded(upto):
        while state['loaded'] < min(upto, len(piece_seq)):
            k = state['loaded']
            i, li, name = piece_seq[k]
            s = k % NSLOT
            n = 2048 if name == 'wp' else 4096
            op('sp', DMA(W[s][:, 0:n], scr[li][PID[name]][:, 0:n]), reads=[('scr', li, name)], writes=[('W', s)],
               dma_slot='w%d' % s)
            state['loaded'] += 1

    def use_piece(i, li, name):
        k = state['used']
        assert piece_seq[k] == (i, li, name), (piece_seq[k], (i, li, name))
        ensure_loaded(k + NSLOT)
        state['used'] += 1
        return k % NSLOT

    def next_acc():
        bank, key = next_bank()
        return 0, bank[:, 0:T], key

    out_ops = []

    def fm_proj(s, c, rhs_fn, rhs_keys, nk=8):
        a, acc, akey = next_acc()
        Wv = W[s].rearrange("p (kc c) -> p kc c", kc=8)
        for kc in range(nk):
            op('pe', MM(acc, Wv[:, kc, c * 128:(c + 1) * 128], rhs_fn(kc), kc == 0, kc == nk - 1),
               reads=[('W', s)] + rhs_keys, writes=[akey])
        return acc, akey

    def DMAT(o_, i_):
        return lambda e: e.dma_start_transpose(out=o_, in_=i_)

    def fm_proj2(s, c0, rhs_fn, rhs_keys, nk=8):
        bank, key = next_bank()
        Wv = W[s].rearrange("p (kc c) -> p kc c", kc=8)
        for cc in range(2):
            c = c0 + cc
            for kc in range(nk):
                op('pe', MM(bank[:, cc * T:(cc + 1) * T], Wv[:, kc, c * 128:(c + 1) * 128], rhs_fn(kc), kc == 0, kc == nk - 1),
                   reads=[('W', s)] + rhs_keys, writes=[key])
        return bank, key

    def build_xT(par, b):
        xT, xbf = xT2[par], xbf2[par]
        if OPT['cast_eng'] == 'act':
            op('act', ACT(xbf[:, b, :], xres[par][b], AF.Copy), reads=[('xres', par, b)], writes=[('xbf', par, b)])
        else:
            op(OPT['cast_eng'], CP(xbf[:, b, :], xres[par][b]), reads=[('xres', par, b)], writes=[('xbf', par, b)])
        op('act' if (OPT['dmat_act'] and OPT['cast_eng'] == 'act') else 'sp',
           DMAT(xT[:, :, b * 128:(b + 1) * 128], xbf[:, b, :].rearrange("t (kc d) -> t kc d", kc=8)),
           reads=[('xbf', par, b)], writes=[('xT', par, b)], dma_slot='xt%d_%d' % (par, b))

    def phase_A(i, li, par):
        l = layers[li]
        xT = xT2[par]
        xT_keys = [('xT', par, b) for b in range(NB)]
        a, acc, akey = next_acc()
        for kc in range(8):
            op('pe', MM(acc[0:16, :], al_sb[:, li, kc, :], xT[:, kc, :], kc == 0, kc == 7),
               reads=[('al_sb', li)] + xT_keys, writes=[akey])
        op('act', ACT(alT_sb, acc[0:16, :], AF.Copy), reads=[akey], writes=['alT'])
        for hp in range(2):
            bank, bkey = next_bank()
            for hh in range(2):
                h = 2 * hp + hh
                op('pe', MM(bank[:, hh * T:(hh + 1) * T], up_sb[:, li, h * 128:(h + 1) * 128], alT_sb, True, True),
                   reads=[('up_sb', li), 'alT'], writes=[bkey])
            for hh in range(2):
                h = 2 * hp + hh
                op('act', ACT(e_tmp[:, h, :], bank[:, hh * T:(hh + 1) * T], AF.Exp, bias=nb_al[:, li, h:h + 1], scale=-1.0),
                   reads=[bkey, ('nb_al', li)], writes=[('e_tmp', h)])
        for h in range(4):
            op('act', ACT(e_tmp[:, h, :], e_tmp[:, h, :], AF.Ln, bias=one1, scale=1.0),
               reads=[('e_tmp', h), 'one1'], writes=[('e_tmp', h)])
        for h in range(4):
            op('dve', lambda e, h=h: e.tensor_tensor_scan(out=Pc[:, h, :], data0=rmask, data1=e_tmp[:, h, :], initial=0.0,
                                                         op0=ALU.mult, op1=ALU.add),
               reads=['rmask', ('e_tmp', h)], writes=[('Pc', h)])
        for h in range(4):
            op('act', ACT(eG[:, h, :], Pc[:, h, :], AF.Exp, scale=-1.0 / 16.0), reads=[('Pc', h)], writes=[('eG', h)])
            op('act', ACT(enG[:, h, :], Pc[:, h, :], AF.Exp, scale=1.0 / 16.0), reads=[('Pc', h)], writes=[('enG', h)])
        yield
        s = use_piece(i, li, 'q')
        for hp in range(2):
            hs = slice(2 * hp, 2 * hp + 2)
            bank, bkey = fm_proj2(s, 2 * hp, lambda kc: xT[:, kc, :], xT_keys)
            bv = bank.rearrange("p (h t) -> p h t", h=2)
            op('dve', STT(qg[:, hs, :], bv, QSCALE, eG[:, hs, :], ALU.mult, ALU.mult),
               reads=[bkey, ('eG', 2 * hp), ('eG', 2 * hp + 1)], writes=[('qg', 2 * hp), ('qg', 2 * hp + 1)])
            op('dve', STT(qn[:, hs, :], bv, QSCALE, enG[:, hs, :], ALU.mult, ALU.mult),
               reads=[bkey, ('enG', 2 * hp), ('enG', 2 * hp + 1)], writes=[('qn', 2 * hp), ('qn', 2 * hp + 1)])
        yield
        s = use_piece(i, li, 'k')
        for hp in range(2):
            hs = slice(2 * hp, 2 * hp + 2)
            bank, bkey = fm_proj2(s, 2 * hp, lambda kc: xT[:, kc, :], xT_keys)
            bv = bank.rearrange("p (h t) -> p h t", h=2)
            for hh in range(2):
                h = 2 * hp + hh
                op('dve', TT(kn[:, :, h, :], bank[:, hh * T:(hh + 1) * T].rearrange("p (b t) -> p b t", b=NB),
                             enG[:, h, :].rearrange("p (b t) -> p b t", b=NB), ALU.mult),
                   reads=[bkey, ('enG', h)], writes=[('kn', h)])
            op('dve', TT(kg[:, hs, :], bv, eG[:, hs, :], ALU.mult),
               reads=[bkey, ('eG', 2 * hp), ('eG', 2 * hp + 1)], writes=[('kg', 2 * hp), ('kg', 2 * hp + 1)])
        yield
        def knT_dmas():
            for b in range(NB):
                op('sp', DMAT(knT[:, b, :].rearrange("s (h d) -> s h d", h=4), kn[:, b, :, :]),
                   reads=[('kn', h) for h in range(4)], writes=[('knT', b)], dma_slot='knT%d' % b)
        if not OPT['knT_late']:
            knT_dmas()
        yield
        for j in range(2):
            s = use_piece(i, li, 'v%d' % j)
            Wv = W[s].rearrange("p (kc c) -> p kc c", kc=8)
            for b in range(NB):
                bank, bkey = next_bank()
                bkeys = [bkey]
                for kc in range(8):
                    op('pe', MM(bank, xT[:, kc, b * 128:(b + 1) * 128], Wv[:, kc, :], kc == 0, kc == 7),
                       reads=[('W', s), ('xT', par, b)], writes=bkeys)
                v_on_act = (b + j) % 2 == 0 or OPT['copy_act']
                op('act' if v_on_act else 'dve',
                   ACT(v_sb[:, b, j * 512:(j + 1) * 512], bank, AF.Copy) if v_on_act else CP(v_sb[:, b, j * 512:(j + 1) * 512], bank),
                   reads=bkeys, writes=[('v', b, j)])
            yield
        for j in range(2):
            s = use_piece(i, li, 'ga%d' % j)
            for c0 in (0, 2):
                bank, bkey = fm_proj2(s, c0, lambda kc: xT[:, kc, :], xT_keys)
                vc = j * 4 + c0
                op('act', ACT(sga[:, vc:vc + 2, :], bank.rearrange("p (c t) -> p c t", c=2), AF.Silu),
                   reads=[bkey], writes=[('sga', vc), ('sga', vc + 1)])
                if OPT['g_pre']:
                    for v_ in (vc, vc + 1):
                        op('pool', TS(sga[:, v_, :], sga[:, v_, :], gng[:, li, v_:v_ + 1], 0.0, ALU.mult, ALU.add),
                           reads=[('sga', v_), ('gng', li)], writes=[('sga', v_)])
            yield
        if OPT['knT_late']:
            knT_dmas()

    def phase_G(i, li, par):
        def g_scores(b):
            ABv = AB.rearrange("p (h a t) -> p h a t", h=4, a=2)
            for h in range(4):
                for hf in range(2):
                    c = 2 * b + hf
                    cs = slice(c * 64, (c + 1) * 64)
                    rs = slice(hf * 64, (hf + 1) * 64)
                    op('pe', MM(ABv[rs, h, 0, :], kn[:, b, h, hf * 64:(hf + 1) * 64], qg[:, h, cs], True, True),
                       reads=[('kn', h), ('qg', h)], writes=[ABKEY])
                    op('pe', MM(ABv[rs, h, 1, :], kg[:, h, cs], qn[:, h, cs], True, True),
                       reads=[('kg', h), ('qn', h)], writes=[ABKEY])
            op('dve', TT(prod, ABv, mask4, ALU.mult), reads=[ABKEY, 'mask4'], writes=['prod'])
            op('pool', TT(scT, prod[:, :, 0, :], prod[:, :, 1, :], ALU.add), reads=['prod'], writes=['scT'])
            yield

        def g_chunks(b):
            for hf in range(2):
                c = 2 * b + hf
                cs = slice(c * 64, (c + 1) * 64)
                rs = slice(hf * 64, (hf + 1) * 64)
                for h in range(4):
                    o_out = Obk[h // 2][rs, (h % 2) * 256:(h % 2 + 1) * 256]
                    op('pe', MM(o_out, scT[rs, h, :], v_sb[rs, b, h * 256:(h + 1) * 256], True, False),
                       reads=['scT', ('v', b, h // 2)], writes=[('PS', 'O%d' % (h // 2))])
                    op('pe', MM(o_out, qg[:, h, cs], Sbf[:, li, h, :], False, True),
                       reads=[('qg', h), ('Sbf', li)], writes=[('PS', 'O%d' % (h // 2))])
                for h in range(4):
                    op('pe', MM(Ubk[h // 2][:, (h % 2) * 256:(h % 2 + 1) * 256], knT[rs, b, h * 128:(h + 1) * 128],
                                v_sb[rs, b, h * 256:(h + 1) * 256], True, True),
                       reads=[('knT', b), ('v', b, h // 2)], writes=[('PS', 'U%d' % (h // 2))])
                for hp in range(2):
                    op('dve', TT(t1[:, 2 * hp:2 * hp + 2, :], Ubk[hp].rearrange("p (h v) -> p h v", h=2),
                                 S[:, li, 2 * hp:2 * hp + 2, :], ALU.add),
                       reads=[('PS', 'U%d' % hp), ('S', li)], writes=[('t1', hp)])
                col = c * 64 + 63
                for h in range(4):
                    egl = eG[:, h, col:col + 1]
                    op('dve', TS(Sbf[:, li, h, :], t1[:, h, :], egl, None, ALU.mult),
                       reads=[('t1', h // 2), ('eG', h)], writes=[('Sbf', li)])
                    op('pool', TS(S[:, li, h, :], t1[:, h, :], egl, 0.0, ALU.mult, ALU.add),
                       reads=[('t1', h // 2), ('eG', h)], writes=[('S', li)])
                yield

        def g_norm(b):
            for _ in range(OPT['norm_yield']):
                yield
            op('pool', lambda e: e.memset(ss4, 0.0), writes=['ss4'])
            for h in range(4):
                op('act', ACT(junk, Obk[h // 2][:, (h % 2) * 256:(h % 2 + 1) * 256], AF.Square, accum=ss4[:, h:h + 1]),
                   reads=[('PS', 'O%d' % (h // 2))], writes=['junk', 'ss4'])
            op('dve', TS(ms4, ss4, 1.0 / 256.0, EPS, ALU.mult, ALU.add), reads=['ss4'], writes=['ms4'])
            op('pool', TT(rstd4, ms4, nhalf, ALU.pow), reads=['ms4', 'nhalf'], writes=['rstd4'])
            for _ in range(OPT['mid_yield']):
                yield
            for h in range(4):
                o_in = Obk[h // 2][:, (h % 2) * 256:(h % 2 + 1) * 256]
                if (OPT['on_eng'] == 'split' and h < 2) or OPT['on_eng'] == 'act' or OPT['dmat_act']:
                    op('act', ACT(on[:, h * 256:(h + 1) * 256], o_in, AF.Identity, scale=rstd4[:, h:h + 1]),
                       reads=[('PS', 'O%d' % (h // 2)), 'rstd4'], writes=[('on', h)])
                else:
                    op('dve', TS(on[:, h * 256:(h + 1) * 256], o_in, rstd4[:, h:h + 1], None, ALU.mult),
                       reads=[('PS', 'O%d' % (h // 2)), 'rstd4'], writes=[('on', h)])
            yield

        def g_tail(b):
            op('act' if OPT['dmat_act'] else 'sp', DMAT(onT[:, b % 2, :, :], on.rearrange("t (vc v) -> t vc v", vc=8)),
               reads=[('on', h) for h in range(4)], writes=[('onT', b % 2)], dma_slot='onT%d' % (b % 2))
            if OPT['g_pre']:
                dst = sga[:, :, b * 128:(b + 1) * 128]
                op('dve', TT(dst, onT[:, b % 2, :, :], dst, ALU.mult),
                   reads=[('onT', b % 2)] + [('sga', vc) for vc in range(8)], writes=[('sga', vc) for vc in range(8)])
            else:
                for vc in range(8):
                    dst = sga[:, vc, b * 128:(b + 1) * 128]
                    op('dve', STT(dst, onT[:, b % 2, vc, :], gng[:, li, vc:vc + 1], dst, ALU.mult, ALU.mult),
                       reads=[('onT', b % 2), ('gng', li), ('sga', vc)], writes=[('sga', vc)])
            yield

        if OPT['g_pipe'] and NB == 2:
            seq = [g_scores(0), g_chunks(0), g_scores(1), g_norm(0), g_chunks(1), g_tail(0), g_norm(1), g_tail(1)]
        else:
            seq = []
            for b in range(NB):
                seq += [g_scores(b), g_chunks(b), g_norm(b), g_tail(b)]
        for g_ in seq:
            for _ in g_:
                yield

    def phase_C(i, li, par):
        first_tile = (i == 0)
        xT = xT2[par]
        xT_keys = [('xT', par, b) for b in range(NB)]
        for j in range(2):
            s = use_piece(i, li, 'pl%d' % j)
            for c0 in (0, 2):
                bank, bkey = fm_proj2(s, c0, lambda kc: xT[:, kc, :], xT_keys)
                cc = j * 4 + c0
                bv = bank.rearrange("p (c t) -> p c t", c=2)
                if c0 == 0 or OPT['copy_act']:
                    op('act', ACT(uT[:, cc:cc + 2, :], bv, AF.Copy), reads=[bkey], writes=[('uT', cc), ('uT', cc + 1)])
                else:
                    op('dve', CP(uT[:, cc:cc + 2, :], bv), reads=[bkey], writes=[('uT', cc), ('uT', cc + 1)])
                yield
        s = use_piece(i, li, 'wp')
        Wp = W[s][:, 0:2048].rearrange("p (g kc o) -> p g kc o", g=4, kc=2)
        for b in range(NB):
            for gp in range(2):
                bank, bkey = next_bank()
                bkeys = [bkey]
                for gg in range(2):
                    g = gp * 2 + gg
                    for kc in range(2):
                        op('pe', MM(bank[:, gg * 256:(gg + 1) * 256], uT[:, 2 * g + kc, b * 128:(b + 1) * 128],
                                    Wp[:, g, kc, :], kc == 0, kc == 1),
                           reads=[('W', s), ('uT', 2 * g + kc)], writes=bkeys)
                if gp == 0 or OPT['copy_act']:
                    op('act', ACT(m_sb[:, b, gp * 512:(gp + 1) * 512], bank, AF.Copy), reads=bkeys, writes=[('m', b, gp)])
                else:
                    op('dve', CP(m_sb[:, b, gp * 512:(gp + 1) * 512], bank), reads=bkeys, writes=[('m', b, gp)])
            yield
        for j in range(2):
            s = use_piece(i, li, 'gb%d' % j)
            for c0 in (0, 2):
                oc0 = j * 4 + c0
                g = oc0 // 2
                bank, bkey = fm_proj2(s, c0, lambda kc: xT[:, kc, :], xT_keys)
                op('act', ACT(sgb[:, oc0:oc0 + 2, :], bank.rearrange("p (c t) -> p c t", c=2), AF.Silu),
                   reads=[bkey], writes=[('sgb', oc0), ('sgb', oc0 + 1)])
                bank2, bkey2 = next_bank()
                for cc in range(2):
                    oc = oc0 + cc
                    for b in range(NB):
                        first = first_tile and b == 0
                        o_sl = bank2[:, cc * T + b * 128:cc * T + (b + 1) * 128]
                        cur = m_sb[:, b, oc * 128:(oc + 1) * 128]
                        op('pe', MM(o_sl, cur, pm[:, (8 + g) if first else g, :], True, first),
                           reads=[('m', b, oc // 4), 'pm'], writes=[bkey2])
                        if not first:
                            if b > 0:
                                prv, pkey = m_sb[:, b - 1, oc * 128:(oc + 1) * 128], ('m', b - 1, oc // 4)
                            else:
                                prv, pkey = mprev[:, li, oc * 128:(oc + 1) * 128], ('mprev', li)
                            op('pe', MM(o_sl, prv, pm[:, 4 + g, :], False, True), reads=[pkey, 'pm'], writes=[bkey2])
                for cc in range(2):
                    oc = oc0 + cc
                    op('dve', STT(sgb[:, oc, :], bank2[:, cc * T:(cc + 1) * T], psc[:, li, oc:oc + 1], sgb[:, oc, :], ALU.mult, ALU.mult),
                       reads=[bkey2, ('psc', li), ('sgb', oc)], writes=[('sgb', oc)])
                yield
        op('pool', CP(mprev[:, li, :], m_sb[:, NB - 1, :]), reads=[('m', NB - 1, 0), ('m', NB - 1, 1)], writes=[('mprev', li)])
        for j in range(4):
            s = use_piece(i, li, 'ml%d' % j)
            for c0 in (0, 2):
                bank, bkey = fm_proj2(s, c0, lambda kc: xT[:, kc, :], xT_keys)
                for cc in range(2):
                    gi = j * 4 + c0 + cc
                    op('act', ACT(gate[:, gi, :], bank[:, cc * T:(cc + 1) * T], AF.Sigmoid, bias=bmg[:, li, gi:gi + 1], scale=1.0),
                       reads=[bkey, ('bmg', li)], writes=[('gate', gi)])
                yield

    def phase_D(i, li, par):
        l = layers[li]
        last_layer = (li == NL - 1)
        op('sp', DMA(lngb[:, 0, :], lng_in[l:l + 1, :].partition_broadcast(128)), writes=[('lngb', 0)], dma_slot='lngb0')
        op('sp', DMA(lngb[:, 1, :], lnb_in[l:l + 1, :].partition_broadcast(128)), writes=[('lngb', 1)], dma_slot='lngb1')
        sga_keys = [('sga', vc) for vc in range(8)]
        sgb_keys = [('sgb', vc) for vc in range(8)]
        first_nm, second_nm = ('pb', 'pa') if OPT['pb_first'] else ('pa', 'pb')
        src = {'pa': (sga, sga_keys, 0), 'pb': (sgb, sgb_keys, 8)}
        f_buf, f_keys, f_g = src[first_nm]
        s_buf, s_keys, s_g = src[second_nm]
        for j in range(2):
            s = use_piece(i, li, '%s%d' % (first_nm, j))
            for c0 in (0, 2):
                dc = j * 4 + c0
                bank, bkey = fm_proj2(s, c0, lambda kc: f_buf[:, kc, :], f_keys)
                op('dve', TT(tA[:, dc:dc + 2, :], bank.rearrange("p (c t) -> p c t", c=2), gate[:, f_g + dc:f_g + dc + 2, :], ALU.mult),
                   reads=[bkey, ('gate', f_g + dc), ('gate', f_g + dc + 1)], writes=[('tA', dc), ('tA', dc + 1)])
            yield
        for j in range(2):
            s = use_piece(i, li, '%s%d' % (second_nm, j))
            for c0 in (0, 2):
                dc = j * 4 + c0
                bank, bkey = fm_proj2(s, c0, lambda kc: s_buf[:, kc, :], s_keys)
                tb = tB[(dc // 2) % 2]
                tkey = ('tB', (dc // 2) % 2)
                op('dve', TT(tb, bank.rearrange("p (c t) -> p c t", c=2), gate[:, s_g + dc:s_g + dc + 2, :], ALU.mult),
                   reads=[bkey, ('gate', s_g + dc), ('gate', s_g + dc + 1)], writes=[tkey])
                op(OPT['add_eng'], TT(mergedT[:, dc:dc + 2, :], tb, tA[:, dc:dc + 2, :], ALU.add), reads=[tkey, ('tA', dc), ('tA', dc + 1)],
                   writes=[('mg', dc), ('mg', dc + 1)])
            yield
        mg_keys = [('mg', dc) for dc in range(8)]
        Ybanks = [Obk, Ubk]
        Ykeys = [[('PS', 'O0'), ('PS', 'O1')], [('PS', 'U0'), ('PS', 'U1')]]
        for j in range(2):
            s = use_piece(i, li, 'wo%d' % j)
            Wv = W[s].rearrange("p (kc c) -> p kc c", kc=8)
            for b in range(NB):
                for dc in range(8):
                    op('pe', MM(Ybanks[b % 2][j], mergedT[:, dc, b * 128:(b + 1) * 128], Wv[:, dc, :], dc == 0, dc == 7),
                       reads=[('W', s)] + mg_keys, writes=[Ykeys[b % 2][j]])
            yield
        if not OPT['ln_batch']:
            for b in range(NB):
                Yb = Ybanks[b % 2]
                ykeys = Ykeys[b % 2]
                xr = xres[par][b]
                xk = ('xres', par, b)
                for j in range(2):
                    xh = xr[:, j * 512:(j + 1) * 512]
                    op('dve', STT(xh, xh, ALPHA, Yb[j], ALU.mult, ALU.add), reads=[xk, ykeys[j]], writes=[xk])
                for j in range(2):
                    op('dve', lambda e, j=j, xr=xr: e.bn_stats(out=bst[:, j, :], in_=xr[:, j * 512:(j + 1) * 512]),
                       reads=[xk], writes=['bst'])
                op('dve', lambda e: e.bn_aggr(out=mv, in_=bst.rearrange("p a s -> p (a s)")), reads=['bst'], writes=['mv'])
                op('dve', TS(vpe, mv[:, 1:2], EPS, None, ALU.add), reads=['mv'], writes=['vpe'])
                op('pool', TT(rstd1, vpe, nhalf[:, 0:1], ALU.pow), reads=['vpe', 'nhalf'], writes=['rstd1'])
                op('dve', TS(xr, xr, mv[:, 0:1], rstd1, ALU.subtract, ALU.mult), reads=[xk, 'mv', 'rstd1'], writes=[xk])
                op('pool', TT(xr, xr, lngb[:, 0, :], ALU.mult), reads=[xk, ('lngb', 0)], writes=[xk])
                op('pool', TT(xr, xr, lngb[:, 1, :], ALU.add), reads=[xk, ('lngb', 1)], writes=[xk])
                if last_layer:
                    r0 = i * T + b * 128
                    out_ops.append(op('sp', DMA(out[r0:r0 + 128, :], xr), reads=[xk], dma_slot='out%d_%d' % (par, b)))
                elif OPT['defer']:
                    deferred.append((par, b))
                else:
                    build_xT(par, b)
                yield
        else:
            for b in range(NB):
                Yb = Ybanks[b % 2]
                ykeys = Ykeys[b % 2]
                xr = xres[par][b]
                xk = ('xres', par, b)
                for j in range(2):
                    xh = xr[:, j * 512:(j + 1) * 512]
                    op('dve', STT(xh, xh, ALPHA, Yb[j], ALU.mult, ALU.add), reads=[xk, ykeys[j]], writes=[xk])
                for j in range(2):
                    op('dve', lambda e, j=j, xr=xr, b=b: e.bn_stats(out=bst2[:, b, j, :], in_=xr[:, j * 512:(j + 1) * 512]),
                       reads=[xk], writes=[('bst', b)])
                op('dve', lambda e, b=b: e.bn_aggr(out=mv2[:, b, :], in_=bst2[:, b, :, :].rearrange("p a s -> p (a s)")),
                   reads=[('bst', b)], writes=[('mv', b)])
            op('dve', TS(vpe2, mv2[:, :, 1], EPS, None, ALU.add), reads=[('mv', b) for b in range(NB)], writes=['vpe2'])
            op('pool', TT(rstd2, vpe2, nhalf[:, 0:NB], ALU.pow), reads=['vpe2', 'nhalf'], writes=['rstd2'])
            yield
            for b in range(NB):
                xr = xres[par][b]
                xk = ('xres', par, b)
                op('dve', TS(xr, xr, mv2[:, b, 0:1], rstd2[:, b:b + 1], ALU.subtract, ALU.mult), reads=[xk, ('mv', b), 'rstd2'], writes=[xk])
            for b in range(NB):
                xr = xres[par][b]
                xk = ('xres', par, b)
                op('pool', TT(xr, xr, lngb[:, 0, :], ALU.mult), reads=[xk, ('lngb', 0)], writes=[xk])
                op('pool', TT(xr, xr, lngb[:, 1, :], ALU.add), reads=[xk, ('lngb', 1)], writes=[xk])
                if last_layer:
                    r0 = i * T + b * 128
                    out_ops.append(op('sp', DMA(out[r0:r0 + 128, :], xr), reads=[xk], dma_slot='out%d_%d' % (par, b)))
                else:
                    build_xT(par, b)
            yield

    def load_x(i):
        par = i % 2
        for b in range(NB):
            r0 = i * T + b * 128
            op('sp', DMA(xres[par][b], x_in[r0:r0 + 128, :]), writes=[('xres', par, b)], dma_slot='xin%d_%d' % (par, b))

    def run_all(*gens):
        for g in gens:
            for _ in g:
                pass

    def run_interleaved(ga, gb, ratio=None):
        ratio = ratio or OPT['ratio']
        done_a = done_b = False
        while not (done_a and done_b):
            if not done_a:
                try:
                    next(ga)
                except StopIteration:
                    done_a = True
            for _ in range(ratio):
                if not done_b:
                    try:
                        next(gb)
                    except StopIteration:
                        done_b = True

    deferred = []

    def flush_deferred():
        pend = list(deferred)
        del deferred[:]
        for (p_, b_) in pend:
            build_xT(p_, b_)

    def tile_layer(i, li, par):
        set_pool(['P0', 'P1', 'P2', 'AB', 'O0', 'O1', 'U0', 'U1'])
        pg.tag = 'PH_A_%d_%d' % (i, li)
        if [d for d in deferred if d[0] == par]:
            flush_deferred()
        run_all(phase_A(i, li, par))
        flush_deferred()
        pg.tag = 'PH_GC_%d_%d' % (i, li)
        set_pool(['P0', 'P1', 'P2', 'AB'] if OPT['ab_alias'] else ['P0', 'P1', 'P2'])
        if interleave:
            run_interleaved(phase_G(i, li, par), phase_C(i, li, par))
        else:
            run_all(phase_G(i, li, par), phase_C(i, li, par))
        set_pool(['P0', 'P1', 'P2', 'AB'])
        pg.tag = 'PH_D_%d_%d' % (i, li)
        run_all(phase_D(i, li, par))

    assert NT % 2 == 0
    set_pool(['P0', 'P1', 'P2', 'AB', 'O0', 'O1', 'U0', 'U1'])
    for m in range(NT // 2):
        tiles = (2 * m, 2 * m + 1)
        for par, i in enumerate(tiles):
            if m == 0:
                load_x(i)
            if m == 0 or not OPT['pair_defer']:
                for b in range(NB):
                    build_xT(par, b)
        for li in range(NL):
            for par, i in enumerate(tiles):
                tile_layer(i, li, par)
                if li == NL - 1 and m + 1 < NT // 2:
                    load_x(i + 2)
                    if OPT['pair_defer']:
                        deferred.extend((par, b) for b in range(NB))

    pg.emit(final_wait_ops=out_ops)
    return nc, pg


PARAM_NAMES = ["w_in", "w_alpha_up", "b_alpha", "gla_norm_g", "w_pool_grp", "pool_scale", "b_merge",
               "w_proj_a", "w_proj_b", "w_out", "ln_g", "ln_b"]


def _prep_params(inputs):
    p = {}
    for k in PARAM_NAMES:
        a = np.ascontiguousarray(np.asarray(inputs[k], dtype=np.float32))
        if k == "gla_norm_g":
            a = a.reshape(DEPTH, 1024)
        p[k] = a
    p.update(make_consts())
    return p


_NC_CACHE = {}


def kernel(**inputs):
    x = np.ascontiguousarray(np.asarray(inputs["x"], dtype=np.float32))
    params = _prep_params(inputs)
    key = (x.shape[1], (0, 1, 2, 3))
    if key not in _NC_CACHE:
        _NC_CACHE[key] = build_nc(x.shape[1], (0, 1, 2, 3))[0]
    nc = _NC_CACHE[key]
    n = 8
    consts = make_consts()
    zero_map = {k: np.zeros_like(v) for k, v in params.items() if k not in consts}
    zero_map.update(consts)
    zero_map["x"] = np.zeros_like(x[0])
    in_maps = []
    for c in range(n):
        if c % 2 == 0:
            m = dict(params)
            m["x"] = x[c // 2]
        else:
            m = zero_map
        in_maps.append(m)
    res = run_bass_kernel_spmd(nc, in_maps, core_ids=list(range(n)))
    return np.stack([res.results[2 * b]["out"] for b in range(BATCH)], axis=0).astype(np.float32)
```

```python
import contextlib
import numpy as np
import concourse.bass as bass
import concourse.mybir as mybir
from concourse.bass_utils import run_bass_kernel_spmd

F32 = mybir.dt.float32
BF16 = mybir.dt.bfloat16
AF = mybir.ActivationFunctionType
ALU = mybir.AluOpType

D = 1024
INC = 7184
SEQ = 8192
BATCH = 4
DEPTH = 4
T = 256
NB = T // 128
NCH = T // 64
ALPHA = float((2.0 * DEPTH) ** 0.25)
EPS = 1e-5
QSCALE = float(128 ** -0.5)
NSLOT = 4
OPT = dict(cast_eng='act', ab_alias=True, ln_batch=False, ratio=2, defer=True, pb_first=True, add_eng='dve', dmat_act=False, knT_late=True, pair_defer=True, norm_yield=0, on_eng='split', mid_yield=0, copy_act=False, g_pre=False, g_pipe=True, dmat_early=False, tail_late=False, head_keys=False, c_first=0, early_g=False, g_pipe2=True)
WIN = (2, 4, 8, 16)

PIECE_ORDER = ['q', 'k', 'v0', 'v1', 'ga0', 'ga1', 'pl0', 'pl1', 'wp', 'gb0', 'gb1', 'ml0', 'ml1', 'ml2', 'ml3',
               'pa0', 'pa1', 'pb0', 'pb1', 'wo0', 'wo1']
WIN_OFF = {'q': 0, 'k': 512, 'v0': 1024, 'v1': 1536, 'ga0': 2048, 'ga1': 2560, 'pl0': 3088, 'pl1': 3600,
           'gb0': 4112, 'gb1': 4624, 'ml0': 5136, 'ml1': 5648, 'ml2': 6160, 'ml3': 6672}
AL_OFF = 3072
PID = {n: i for i, n in enumerate(PIECE_ORDER)}


class Prog:
    def __init__(self, nc):
        self.nc = nc
        self.ops = []
        self.last_w = {}
        self.readers = {}

    def op(self, eng, fn, reads=(), writes=(), dma_slot=None):
        idx = len(self.ops)
        deps = set()
        for r in reads:
            lw = self.last_w.get(r)
            if lw is not None:
                deps.add(lw)
            if isinstance(r, tuple) and r[0] == 'PS':
                last = {}
                for rd in self.readers.get(r, ()):
                    e_ = self.ops[rd]['eng']
                    if e_ != eng and rd > last.get(e_, -1):
                        last[e_] = rd
                deps.update(last.values())
        for w in writes:
            lw = self.last_w.get(w)
            if lw is not None:
                deps.add(lw)
            rl = self.readers.get(w)
            if rl:
                last = {}
                for rd in rl:
                    o_ = self.ops[rd]
                    if o_['dma'] is not None:
                        deps.add(rd)
                    elif rd > last.get(o_['eng'], -1):
                        last[o_['eng']] = rd
                deps.update(last.values())
        deps.discard(idx)
        self.ops.append(dict(eng=eng, fn=fn, deps=deps, dma=dma_slot, idx=idx, tag=getattr(self, 'tag', None), r=list(reads), w=list(writes)))
        for w in writes:
            self.last_w[w] = idx
            self.readers[w] = []
        ws = set(writes)
        for r in reads:
            if r not in ws:
                self.readers.setdefault(r, []).append(idx)
        return idx

    def emit(self, final_wait_ops=()):
        nc = self.nc
        ops = self.ops
        engs = ['pe', 'act', 'dve', 'pool', 'sp']

        def is_pe_pe(a, b):
            return a['eng'] == 'pe' and b['eng'] == 'pe' and a['dma'] is None and b['dma'] is None

        needed = set(final_wait_ops)
        for o in ops:
            for d in o['deps']:
                if not is_pe_pe(ops[d], o):
                    needed.add(d)
        tick = {e: 0 for e in engs}
        slot_cnt = {}
        for o in ops:
            if o['dma'] is not None:
                s = o['dma']
                slot_cnt[s] = slot_cnt.get(s, 0) + 16
                o['sem'] = ('dma', s)
                o['val'] = slot_cnt[s]
                o['inc'] = True
            else:
                if o['idx'] in needed:
                    tick[o['eng']] += 1
                    o['inc'] = True
                else:
                    o['inc'] = False
                o['sem'] = ('eng', o['eng'])
                o['val'] = tick[o['eng']]
        for o in ops:
            if o['dma'] is not None and str(o['dma']).startswith('G:'):
                o['val'] = slot_cnt[o['dma']]
        sem_names = sorted(set(o['sem'] for o in ops), key=str)
        self.n_sems = len(sem_names)
        self.nwaits = 0
        with contextlib.ExitStack() as st:
            sems = {}
            for i, sn in enumerate(sem_names):
                sems[sn] = st.enter_context(nc.semaphore("sem%d" % i))
            self.sem_map = {str(k): str(v) for k, v in sems.items()}
            block = st.enter_context(nc.Block())
            per_eng = {e: [o for o in ops if o['eng'] == e] for e in engs}

            def body(e, engine):
                known = {}
                for o in per_eng[e]:
                    req = {}
                    for d in o['deps']:
                        do = ops[d]
                        if is_pe_pe(do, o):
                            continue
                        k = do['sem']
                        if do['val'] > req.get(k, 0):
                            req[k] = do['val']
                    for k, v in req.items():
                        if known.get(k, 0) >= v:
                            continue
                        engine.wait_ge(sems[k], v)
                        self.nwaits += 1
                        known[k] = v
                    ins = o['fn'](engine)
                    try:
                        o['iname'] = ins.ins.name
                    except Exception:
                        o['iname'] = None
                    if o.get('tag') and getattr(self, 'annotate', False):
                        ins.annotate(o['tag'])
                    if o['inc']:
                        ins.then_inc(sems[o['sem']], 16 if o['dma'] is not None else 1)
                if e == 'sp':
                    fin = {}
                    for f in final_wait_ops:
                        fo = ops[f]
                        fin[fo['sem']] = max(fin.get(fo['sem'], 0), fo['val'])
                    for k, v in fin.items():
                        engine.wait_ge(sems[k], v)

            @block.tensor
            def _(eng):
                body('pe', eng)

            @block.scalar
            def _(eng):
                body('act', eng)

            @block.vector
            def _(eng):
                body('dve', eng)

            @block.gpsimd
            def _(eng):
                body('pool', eng)

            @block.sync
            def _(eng):
                body('sp', eng)


def make_consts():
    ident = np.eye(128, dtype=np.float32)
    s = np.arange(128)[:, None] % 64
    t = np.arange(64)[None, :]
    mA = (s <= t).astype(np.float32)
    mB = (s > t).astype(np.float32)
    m4 = np.zeros((128, 4, 2, 64), np.float32)
    m4[:, :, 0, :] = mA[:, None, :]
    m4[:, :, 1, :] = mB[:, None, :]
    rmask = np.ones((128, T), np.float32)
    rmask[:, ::64] = 0.0
    pm = np.zeros((128, 12, 128), np.float32)
    ss = np.arange(128)[:, None]
    tt = np.arange(128)[None, :]
    for g, w in enumerate(WIN):
        dcur = tt - ss
        pm[:, g, :] = ((dcur >= 0) & (dcur < w)) / float(w) - (dcur == 0)
        dprev = tt + 128 - ss
        pm[:, 4 + g, :] = ((dprev >= 0) & (dprev < w)) / float(w)
        cnt = np.minimum(tt + 1, w).astype(np.float32)
        pm[:, 8 + g, :] = ((dcur >= 0) & (dcur < w)) / cnt - (dcur == 0)
    return dict(c_ident=ident, c_mask=m4.reshape(128, 512), c_rmask=rmask, c_pm=pm.reshape(128, 12 * 128))


def build_nc(seq_len=SEQ, layers=(0, 1, 2, 3), interleave=True, annotate=False):
    nc = bass.Bass("TRN2", target_bir_lowering=False)
    NL = len(layers)
    NT = seq_len // T

    def din(name, shape):
        return nc.dram_tensor(name, list(shape), F32, kind="ExternalInput").ap()

    x_in = din("x", [seq_len, D])
    w_in = din("w_in", [DEPTH, D, INC])
    w_up = din("w_alpha_up", [DEPTH, 16, 512])
    b_al = din("b_alpha", [DEPTH, 512])
    gng_in = din("gla_norm_g", [DEPTH, 1024])
    wpg = din("w_pool_grp", [DEPTH, 4, 256, 256])
    psc_in = din("pool_scale", [DEPTH, 1024])
    bmg_in = din("b_merge", [DEPTH, 2048])
    w_pa = din("w_proj_a", [DEPTH, D, D])
    w_pb = din("w_proj_b", [DEPTH, D, D])
    w_wo = din("w_out", [DEPTH, D, D])
    lng_in = din("ln_g", [DEPTH, D])
    lnb_in = din("ln_b", [DEPTH, D])
    c_ident = din("c_ident", [128, 128])
    c_mask = din("c_mask", [128, 512])
    c_rmask = din("c_rmask", [128, T])
    c_pm = din("c_pm", [128, 12 * 128])
    out = nc.dram_tensor("out", [seq_len, D], F32, kind="ExternalOutput").ap()
    scr = [nc.dram_tensor("scr%d" % li, [len(PIECE_ORDER), 128, 4096], BF16).ap() for li in range(NL)]

    def SB(name, shape, dt):
        return nc.alloc_sbuf_tensor(name, list(shape), dt).ap()

    def PS(name, shape):
        return nc.alloc_psum_tensor(name, list(shape), F32).ap()

    xres = [[SB("xres%d_%d" % (p, b), [128, D], F32) for b in range(NB)] for p in range(2)]
    xT2 = [SB("xT%d" % p, [128, 8, T], BF16) for p in range(2)]
    xbf2 = [SB("xbf%d" % p, [128, NB, D], BF16) for p in range(2)]
    onT = SB("onT", [128, 2, 8, 128], BF16)
    W = [SB("W%d" % s, [128, 4096], BF16) for s in range(NSLOT)]
    al_sb = SB("al_sb", [128, NL, 8, 16], BF16)
    up_sb = SB("up_sb", [16, NL, 512], BF16)
    alT_sb = SB("alT_sb", [16, T], BF16)
    e_tmp = SB("e_tmp", [128, 4, T], F32)
    Pc = SB("Pc", [128, 4, T], F32)
    eG = SB("eG", [128, 4, T], F32)
    enG = SB("enG", [128, 4, T], F32)
    qg = SB("qg", [128, 4, T], BF16)
    qn = SB("qn", [128, 4, T], BF16)
    kn = SB("kn", [128, NB, 4, 128], BF16)
    kg = SB("kg", [128, 4, T], BF16)
    knT = SB("knT", [128, NB, 512], BF16)
    v_sb = SB("v_sb", [128, NB, 1024], BF16)
    sga = SB("sga", [128, 8, T], BF16)
    sgb = SB("sgb", [128, 8, T], BF16)
    uT = SB("uT", [128, 8, T], BF16)
    m_sb = SB("m_sb", [128, NB, 1024], BF16)
    mprev = SB("mprev", [128, NL, 1024], BF16)
    gate = SB("gate", [128, 16, T], BF16)
    tA = SB("tA", [128, 8, T], F32)
    tB = [SB("tB%d" % i, [128, 2, T], F32) for i in range(2)]
    mergedT = SB("mergedT", [128, 8, T], BF16)
    prod = SB("prod", [128, 4, 2, 64], F32)
    scT = SB("scT", [128, 4, 64], BF16)
    on_b = [SB("on%d" % i_, [128, 1024], BF16) for i_ in range(2)]
    junk = SB("junk", [128, 256], BF16)
    t1 = SB("t1", [128, 4, 256], F32)
    S = SB("S", [128, NL, 4, 256], F32)
    Sbf = SB("Sbf", [128, NL, 4, 256], BF16)
    ss4 = SB("ss4", [128, 4], F32)
    ms4 = SB("ms4", [128, 4], F32)
    rstd4 = SB("rstd4", [128, 4], F32)
    lngb = SB("lngb", [128, 2, D], F32)
    bst = SB("bst", [128, 2, 6], F32)
    mv = SB("mv", [128, 2], F32)
    vpe = SB("vpe", [128, 1], F32)
    rstd1 = SB("rstd1", [128, 1], F32)
    bst2 = SB("bst2", [128, NB, 2, 6], F32)
    mv2 = SB("mv2", [128, NB, 2], F32)
    vpe2 = SB("vpe2", [128, NB], F32)
    rstd2 = SB("rstd2", [128, NB], F32)
    ident_f = SB("ident_f", [128, 128], F32)
    ident_b = SB("ident_b", [128, 128], BF16)
    mask4 = SB("mask4", [128, 4, 2, 64], F32)
    rmask = SB("rmask", [128, T], F32)
    pm = SB("pm", [128, 12, 128], BF16)
    nhalf = SB("nhalf", [128, 4], F32)
    one1 = SB("one1", [128, 1], F32)
    nb_al = SB("nb_al", [128, NL, 4], F32)
    gng = SB("gng", [128, NL, 8], F32)
    psc = SB("psc", [128, NL, 8], F32)
    bmg = SB("bmg", [128, NL, 16], F32)

    BANKS = {n: PS(n, [128, 512]) for n in ['P0', 'P1', 'P2', 'AB', 'O0', 'O1', 'U0', 'U1']}
    AB = BANKS['U0'] if OPT['ab_alias'] else BANKS['AB']
    ABKEY = ('PS', 'U0') if OPT['ab_alias'] else ('PS', 'AB')
    Obk = [BANKS['O0'], BANKS['O1']]
    Ubk = [BANKS['U0'], BANKS['U1']]
    pool_state = dict(names=['P0', 'P1', 'P2'], n=0)

    def set_pool(names):
        pool_state['names'] = list(names)

    def next_bank():
        nm = pool_state['names'][pool_state['n'] % len(pool_state['names'])]
        pool_state['n'] += 1
        return BANKS[nm], ('PS', nm)

    pg = Prog(nc)
    pg.annotate = annotate
    op = pg.op

    def MM(o_, l_, r_, st, sp):
        return lambda e: e.matmul(o_, lhsT=l_, rhs=r_, start=st, stop=sp)

    def TR(o_, i_, id_):
        return lambda e: e.transpose(o_, i_, id_)

    def ACT(o_, i_, f, bias=None, scale=None, accum=None):
        kw = {}
        if bias is not None:
            kw['bias'] = bias
        if scale is not None:
            kw['scale'] = scale
        if accum is not None:
            kw['accum_out'] = accum
        return lambda e: e.activation(out=o_, in_=i_, func=f, **kw)

    def TT(o_, a, b, o):
        return lambda e: e.tensor_tensor(out=o_, in0=a, in1=b, op=o)

    def TS(o_, a, s1, s2, o0, o1=None):
        if o1 is None:
            return lambda e: e.tensor_scalar(out=o_, in0=a, scalar1=s1, scalar2=s2, op0=o0)
        return lambda e: e.tensor_scalar(out=o_, in0=a, scalar1=s1, scalar2=s2, op0=o0, op1=o1)

    def STT(o_, a, s, b, o0, o1):
        return lambda e: e.scalar_tensor_tensor(out=o_, in0=a, scalar=s, in1=b, op0=o0, op1=o1)

    def DMA(o_, i_):
        return lambda e: e.dma_start(out=o_, in_=i_)

    def DMAs(o_, i_):
        return lambda e: e.dma_start(out=o_, in_=i_, allow_slow_non_contiguous=True)

    def CP(o_, i_):
        return lambda e: e.tensor_copy(out=o_, in_=i_)

    op('sp', DMA(ident_f, c_ident), writes=['ident_f'], dma_slot='G:c')
    op('sp', DMA(mask4, c_mask.rearrange("p (h a t) -> p h a t", h=4, a=2)), writes=['mask4'], dma_slot='G:c')
    op('sp', DMA(rmask, c_rmask), writes=['rmask'], dma_slot='G:c')
    for li, l in enumerate(layers):
        op('sp', DMAs(nb_al[:, li, :], b_al[l].rearrange("(h p) -> p h", p=128)), writes=[('nb_al', li)], dma_slot='G:c')
        op('sp', DMAs(gng[:, li, :], gng_in[l].rearrange("(c p) -> p c", p=128)), writes=[('gng', li)], dma_slot='G:c')
        op('sp', DMAs(psc[:, li, :], psc_in[l].rearrange("(c p) -> p c", p=128)), writes=[('psc', li)], dma_slot='G:c')
        op('sp', DMAs(bmg[:, li, :], bmg_in[l].rearrange("(c p) -> p c", p=128)), writes=[('bmg', li)], dma_slot='G:c')
    op('pool', DMA(pm, c_pm.rearrange("p (k t) -> p k t", k=12)), writes=['pm'], dma_slot='G:cp')
    for li, l in enumerate(layers):
        op('pool', DMA(al_sb[:, li, :, :], w_in[l][:, AL_OFF:AL_OFF + 16].rearrange("(kc p) c -> p kc c", p=128)),
           writes=[('al_sb', li)], dma_slot='G:cp')
        op('pool', DMA(up_sb[:, li, :], w_up[l]), writes=[('up_sb', li)], dma_slot='G:cp')
    for li, l in enumerate(layers):
        for name in PIECE_ORDER:
            pid = PID[name]
            grp = 'G:w%d_%d' % (li, 0 if pid < 8 else (1 if pid < 15 else 2))
            if name in WIN_OFF:
                c0 = WIN_OFF[name]
                src = w_in[l][:, c0:c0 + 512].rearrange("(kc p) c -> p kc c", p=128)
                dst = scr[li][pid].rearrange("p (kc c) -> p kc c", kc=8)
            elif name == 'wp':
                src = wpg[l].rearrange("g (kc p) o -> p g kc o", p=128)
                dst = scr[li][pid][:, 0:2048].rearrange("p (g kc o) -> p g kc o", g=4, kc=2)
            else:
                wsrc = {'pa': w_pa, 'pb': w_pb, 'wo': w_wo}[name[:2]]
                c0 = int(name[2]) * 512
                src = wsrc[l][:, c0:c0 + 512].rearrange("(kc p) c -> p kc c", p=128)
                dst = scr[li][pid].rearrange("p (kc c) -> p kc c", kc=8)
            op('pool', DMA(dst, src), writes=[('scr', li, name)], dma_slot=grp)
    op('dve', CP(ident_b, ident_f), reads=['ident_f'], writes=['ident_b'])
    op('dve', lambda e: e.memset(nhalf, -0.5), writes=['nhalf'])
    op('dve', lambda e: e.memset(one1, 1.0), writes=['one1'])
    op('dve', TS(nb_al, nb_al, -1.0, None, ALU.mult), reads=[('nb_al', li) for li in range(NL)], writes=[('nb_al', li) for li in range(NL)])
    op('pool', lambda e: e.memset(S, 0.0), writes=[('S', li) for li in range(NL)] + [('S', li, h) for li in range(NL) for h in range(4)])
    op('pool', lambda e: e.memset(Sbf, 0.0), writes=[('Sbf', li) for li in range(NL)] + [('Sbf', li, h) for li in range(NL) for h in range(4)])

    porder = list(PIECE_ORDER)
    if OPT['pb_first']:
        ia, ib = porder.index('pa0'), porder.index('pb0')
        porder[ia:ia + 2], porder[ib:ib + 2] = ['pb0', 'pb1'], ['pa0', 'pa1']
    piece_seq = [(i, li, name) for m in range(NT // 2) for li in range(NL) for i in (2 * m, 2 * m + 1) for name in porder]
    state = dict(loaded=0, used=0)

    def ensure_loaded(upto):
        while state['loaded'] < min(upto, len(piece_seq)):
            k = state['loaded']
            i, li, name = piece_seq[k]
            s = k % NSLOT
            n = 2048 if name == 'wp' else 4096
            op('sp', DMA(W[s][:, 0:n], scr[li][PID[name]][:, 0:n]), reads=[('scr', li, name)], writes=[('W', s)],
               dma_slot='w%d' % s)
            state['loaded'] += 1

    def use_piece(i, li, name):
        k = state['used']
        assert piece_seq[k] == (i, li, name), (piece_seq[k], (i, li, name))
        ensure_loaded(k + NSLOT)
        state['used'] += 1
        return k % NSLOT

    def next_acc():
        bank, key = next_bank()
        return 0, bank[:, 0:T], key

    out_ops = []

    def fm_proj(s, c, rhs_fn, rhs_keys, nk=8):
        a, acc, akey = next_acc()
        Wv = W[s].rearrange("p (kc c) -> p kc c", kc=8)
        for kc in range(nk):
            op('pe', MM(acc, Wv[:, kc, c * 128:(c + 1) * 128], rhs_fn(kc), kc == 0, kc == nk - 1),
               reads=[('W', s)] + rhs_keys, writes=[akey])
        return acc, akey

    def DMAT(o_, i_):
        return lambda e: e.dma_start_transpose(out=o_, in_=i_)

    def fm_proj2(s, c0, rhs_fn, rhs_keys, nk=8):
        bank, key = next_bank()
        Wv = W[s].rearrange("p (kc c) -> p kc c", kc=8)
        for cc in range(2):
            c = c0 + cc
            for kc in range(nk):
                op('pe', MM(bank[:, cc * T:(cc + 1) * T], Wv[:, kc, c * 128:(c + 1) * 128], rhs_fn(kc), kc == 0, kc == nk - 1),
                   reads=[('W', s)] + rhs_keys, writes=[key])
        return bank, key

    def build_xT(par, b):
        xT, xbf = xT2[par], xbf2[par]
        if OPT['cast_eng'] == 'act':
            op('act', ACT(xbf[:, b, :], xres[par][b], AF.Copy), reads=[('xres', par, b)], writes=[('xbf', par, b)])
        else:
            op(OPT['cast_eng'], CP(xbf[:, b, :], xres[par][b]), reads=[('xres', par, b)], writes=[('xbf', par, b)])
        op('act' if (OPT['dmat_act'] and OPT['cast_eng'] == 'act') else 'sp',
           DMAT(xT[:, :, b * 128:(b + 1) * 128], xbf[:, b, :].rearrange("t (kc d) -> t kc d", kc=8)),
           reads=[('xbf', par, b)], writes=[('xT', par, b)], dma_slot='xt%d_%d' % (par, b))

    def phase_A(i, li, par):
        l = layers[li]
        xT = xT2[par]
        xT_keys = [('xT', par, b) for b in range(NB)]
        a, acc, akey = next_acc()
        for kc in range(8):
            op('pe', MM(acc[0:16, :], al_sb[:, li, kc, :], xT[:, kc, :], kc == 0, kc == 7),
               reads=[('al_sb', li)] + xT_keys, writes=[akey])
        op('act', ACT(alT_sb, acc[0:16, :], AF.Copy), reads=[akey], writes=['alT'])
        for hp in range(2):
            bank, bkey = next_bank()
            for hh in range(2):
                h = 2 * hp + hh
                op('pe', MM(bank[:, hh * T:(hh + 1) * T], up_sb[:, li, h * 128:(h + 1) * 128], alT_sb, True, True),
                   reads=[('up_sb', li), 'alT'], writes=[bkey])
            for hh in range(2):
                h = 2 * hp + hh
                op('act', ACT(e_tmp[:, h, :], bank[:, hh * T:(hh + 1) * T], AF.Exp, bias=nb_al[:, li, h:h + 1], scale=-1.0),
                   reads=[bkey, ('nb_al', li)], writes=[('e_tmp', h)])
        for h in range(4):
            op('act', ACT(e_tmp[:, h, :], e_tmp[:, h, :], AF.Ln, bias=one1, scale=1.0),
               reads=[('e_tmp', h), 'one1'], writes=[('e_tmp', h)])
        for h in range(4):
            op('dve', lambda e, h=h: e.tensor_tensor_scan(out=Pc[:, h, :], data0=rmask, data1=e_tmp[:, h, :], initial=0.0,
                                                         op0=ALU.mult, op1=ALU.add),
               reads=['rmask', ('e_tmp', h)], writes=[('Pc', h)])
        for h in range(4):
            op('act', ACT(eG[:, h, :], Pc[:, h, :], AF.Exp, scale=-1.0 / 16.0), reads=[('Pc', h)], writes=[('eG', h)])
            op('act', ACT(enG[:, h, :], Pc[:, h, :], AF.Exp, scale=1.0 / 16.0), reads=[('Pc', h)], writes=[('enG', h)])
        yield
        s = use_piece(i, li, 'q')
        for hp in range(2):
            hs = slice(2 * hp, 2 * hp + 2)
            bank, bkey = fm_proj2(s, 2 * hp, lambda kc: xT[:, kc, :], xT_keys)
            bv = bank.rearrange("p (h t) -> p h t", h=2)
            op('dve', STT(qg[:, hs, :], bv, QSCALE, eG[:, hs, :], ALU.mult, ALU.mult),
               reads=[bkey, ('eG', 2 * hp), ('eG', 2 * hp + 1)], writes=[('qg', 2 * hp), ('qg', 2 * hp + 1)])
            op('dve', STT(qn[:, hs, :], bv, QSCALE, enG[:, hs, :], ALU.mult, ALU.mult),
               reads=[bkey, ('enG', 2 * hp), ('enG', 2 * hp + 1)], writes=[('qn', 2 * hp), ('qn', 2 * hp + 1)])
        yield
        s = use_piece(i, li, 'k')
        for hp in range(2):
            hs = slice(2 * hp, 2 * hp + 2)
            bank, bkey = fm_proj2(s, 2 * hp, lambda kc: xT[:, kc, :], xT_keys)
            bv = bank.rearrange("p (h t) -> p h t", h=2)
            for hh in range(2):
                h = 2 * hp + hh
                op('dve', TT(kn[:, :, h, :], bank[:, hh * T:(hh + 1) * T].rearrange("p (b t) -> p b t", b=NB),
                             enG[:, h, :].rearrange("p (b t) -> p b t", b=NB), ALU.mult),
                   reads=[bkey, ('enG', h)], writes=[('kn', h)])
            op('dve', TT(kg[:, hs, :], bv, eG[:, hs, :], ALU.mult),
               reads=[bkey, ('eG', 2 * hp), ('eG', 2 * hp + 1)], writes=[('kg', 2 * hp), ('kg', 2 * hp + 1)])
        yield
        def knT_dmas():
            for b in range(NB):
                op('sp', DMAT(knT[:, b, :].rearrange("s (h d) -> s h d", h=4), kn[:, b, :, :]),
                   reads=[('kn', h) for h in range(4)], writes=[('knT', b)], dma_slot='knT%d' % b)
        if not OPT['knT_late'] or OPT['early_g']:
            knT_dmas()
        yield 'SPLIT'
        for j in range(2):
            s = use_piece(i, li, 'v%d' % j)
            Wv = W[s].rearrange("p (kc c) -> p kc c", kc=8)
            for b in range(NB):
                bank, bkey = next_bank()
                bkeys = [bkey]
                for kc in range(8):
                    op('pe', MM(bank, xT[:, kc, b * 128:(b + 1) * 128], Wv[:, kc, :], kc == 0, kc == 7),
                       reads=[('W', s), ('xT', par, b)], writes=bkeys)
                v_on_act = (b + j) % 2 == 0 or OPT['copy_act']
                op('act' if v_on_act else 'dve',
                   ACT(v_sb[:, b, j * 512:(j + 1) * 512], bank, AF.Copy) if v_on_act else CP(v_sb[:, b, j * 512:(j + 1) * 512], bank),
                   reads=bkeys, writes=[('v', b, j)])
            yield
        for j in range(2):
            s = use_piece(i, li, 'ga%d' % j)
            for c0 in (0, 2):
                bank, bkey = fm_proj2(s, c0, lambda kc: xT[:, kc, :], xT_keys)
                vc = j * 4 + c0
                op('act', ACT(sga[:, vc:vc + 2, :], bank.rearrange("p (c t) -> p c t", c=2), AF.Silu),
                   reads=[bkey], writes=[('sga', vc), ('sga', vc + 1)])
                if OPT['g_pre']:
                    for v_ in (vc, vc + 1):
                        op('pool', TS(sga[:, v_, :], sga[:, v_, :], gng[:, li, v_:v_ + 1], 0.0, ALU.mult, ALU.add),
                           reads=[('sga', v_), ('gng', li)], writes=[('sga', v_)])
            yield
        if OPT['knT_late'] and not OPT['early_g']:
            knT_dmas()

    late_tail = []

    def phase_G(i, li, par):
        def g_scores(b):
            ABv = AB.rearrange("p (h a t) -> p h a t", h=4, a=2)
            for h in range(4):
                for hf in range(2):
                    c = 2 * b + hf
                    cs = slice(c * 64, (c + 1) * 64)
                    rs = slice(hf * 64, (hf + 1) * 64)
                    op('pe', MM(ABv[rs, h, 0, :], kn[:, b, h, hf * 64:(hf + 1) * 64], qg[:, h, cs], True, True),
                       reads=[('kn', h), ('qg', h)], writes=[ABKEY])
                    op('pe', MM(ABv[rs, h, 1, :], kg[:, h, cs], qn[:, h, cs], True, True),
                       reads=[('kg', h), ('qn', h)], writes=[ABKEY])
            op('dve', TT(prod, ABv, mask4, ALU.mult), reads=[ABKEY, 'mask4'], writes=['prod'])
            op('pool', TT(scT, prod[:, :, 0, :], prod[:, :, 1, :], ALU.add), reads=['prod'], writes=['scT'])
            yield

        def g_chunks(b):
            for hf in range(2):
                c = 2 * b + hf
                cs = slice(c * 64, (c + 1) * 64)
                rs = slice(hf * 64, (hf + 1) * 64)
                for h in range(4):
                    o_out = Obk[h // 2][rs, (h % 2) * 256:(h % 2 + 1) * 256]
                    op('pe', MM(o_out, scT[rs, h, :], v_sb[rs, b, h * 256:(h + 1) * 256], True, False),
                       reads=['scT', ('v', b, h // 2)], writes=[('PS', 'O%d' % (h // 2))])
                    op('pe', MM(o_out, qg[:, h, cs], Sbf[:, li, h, :], False, True),
                       reads=[('qg', h), (('Sbf', li, h) if OPT['head_keys'] else ('Sbf', li))], writes=[('PS', 'O%d' % (h // 2))])
                for h in range(4):
                    op('pe', MM(Ubk[h // 2][:, (h % 2) * 256:(h % 2 + 1) * 256], knT[rs, b, h * 128:(h + 1) * 128],
                                v_sb[rs, b, h * 256:(h + 1) * 256], True, True),
                       reads=[('knT', b), ('v', b, h // 2)], writes=[('PS', 'U%d' % (h // 2))])
                col = c * 64 + 63
                if not OPT['head_keys']:
                    for hp in range(2):
                        op('dve', TT(t1[:, 2 * hp:2 * hp + 2, :], Ubk[hp].rearrange("p (h v) -> p h v", h=2),
                                     S[:, li, 2 * hp:2 * hp + 2, :], ALU.add),
                           reads=[('PS', 'U%d' % hp), ('S', li)], writes=[('t1', hp)])
                    for h in range(4):
                        egl = eG[:, h, col:col + 1]
                        op('dve', TS(Sbf[:, li, h, :], t1[:, h, :], egl, None, ALU.mult),
                           reads=[('t1', h // 2), ('eG', h)], writes=[('Sbf', li)])
                        op('pool', TS(S[:, li, h, :], t1[:, h, :], egl, 0.0, ALU.mult, ALU.add),
                           reads=[('t1', h // 2), ('eG', h)], writes=[('S', li)])
                else:
                    for hp in range(2):
                        op('dve', TT(t1[:, 2 * hp:2 * hp + 2, :], Ubk[hp].rearrange("p (h v) -> p h v", h=2),
                                     S[:, li, 2 * hp:2 * hp + 2, :], ALU.add),
                           reads=[('PS', 'U%d' % hp), ('S', li, 2 * hp), ('S', li, 2 * hp + 1)], writes=[('t1', hp)])
                        for h in (2 * hp, 2 * hp + 1):
                            egl = eG[:, h, col:col + 1]
                            op('dve', TS(Sbf[:, li, h, :], t1[:, h, :], egl, None, ALU.mult),
                               reads=[('t1', hp), ('eG', h)], writes=[('Sbf', li, h)])
                            op('pool', TS(S[:, li, h, :], t1[:, h, :], egl, 0.0, ALU.mult, ALU.add),
                               reads=[('t1', hp), ('eG', h)], writes=[('S', li, h)])
                yield

        def g_norm(b):
            for _ in range(OPT['norm_yield']):
                yield
            op('pool', lambda e: e.memset(ss4, 0.0), writes=['ss4'])
            for h in range(4):
                op('act', ACT(junk, Obk[h // 2][:, (h % 2) * 256:(h % 2 + 1) * 256], AF.Square, accum=ss4[:, h:h + 1]),
                   reads=[('PS', 'O%d' % (h // 2))], writes=['junk', 'ss4'])
            op('dve', TS(ms4, ss4, 1.0 / 256.0, EPS, ALU.mult, ALU.add), reads=['ss4'], writes=['ms4'])
            op('pool', TT(rstd4, ms4, nhalf, ALU.pow), reads=['ms4', 'nhalf'], writes=['rstd4'])
            for _ in range(OPT['mid_yield']):
                yield
            for h in range(4):
                o_in = Obk[h // 2][:, (h % 2) * 256:(h % 2 + 1) * 256]
                if (OPT['on_eng'] == 'split' and h < 2) or OPT['on_eng'] == 'act' or OPT['dmat_act']:
                    op('act', ACT(on_b[b % 2][:, h * 256:(h + 1) * 256], o_in, AF.Identity, scale=rstd4[:, h:h + 1]),
                       reads=[('PS', 'O%d' % (h // 2)), 'rstd4'], writes=[('on', b % 2, h)])
                else:
                    op('dve', TS(on_b[b % 2][:, h * 256:(h + 1) * 256], o_in, rstd4[:, h:h + 1], None, ALU.mult),
                       reads=[('PS', 'O%d' % (h // 2)), 'rstd4'], writes=[('on', b % 2, h)])
            if OPT['dmat_early']:
                g_dmat(b)
            yield

        def g_dmat(b):
            op('act' if OPT['dmat_act'] else 'sp', DMAT(onT[:, b % 2, :, :], on_b[b % 2].rearrange("t (vc v) -> t vc v", vc=8)),
               reads=[('on', b % 2, h) for h in range(4)], writes=[('onT', b % 2)], dma_slot='onT%d' % (b % 2))

        def stt_ops(b):
            for vc in range(8):
                dst = sga[:, vc, b * 128:(b + 1) * 128]
                op('dve', STT(dst, onT[:, b % 2, vc, :], gng[:, li, vc:vc + 1], dst, ALU.mult, ALU.mult),
                   reads=[('onT', b % 2), ('gng', li), ('sga', vc)], writes=[('sga', vc)])

        def g_tail(b):
            if not OPT['dmat_early']:
                g_dmat(b)
            if OPT['tail_late'] and b == NB - 1:
                late_tail.append(lambda b=b: stt_ops(b))
            else:
                stt_ops(b)
            yield

        if OPT['g_pipe2'] and NB == 2:
            seq = [g_scores(0), g_chunks(0), g_scores(1), g_norm(0), g_chunks(1), g_norm(1), g_tail(0), g_tail(1)]
        elif OPT['g_pipe'] and NB == 2:
            seq = [g_scores(0), g_chunks(0), g_scores(1), g_norm(0), g_chunks(1), g_tail(0), g_norm(1), g_tail(1)]
        else:
            seq = []
            for b in range(NB):
                seq += [g_scores(b), g_chunks(b), g_norm(b), g_tail(b)]
        for g_ in seq:
            for _ in g_:
                yield

    def phase_C(i, li, par):
        first_tile = (i == 0)
        xT = xT2[par]
        xT_keys = [('xT', par, b) for b in range(NB)]
        for j in range(2):
            s = use_piece(i, li, 'pl%d' % j)
            for c0 in (0, 2):
                bank, bkey = fm_proj2(s, c0, lambda kc: xT[:, kc, :], xT_keys)
                cc = j * 4 + c0
                bv = bank.rearrange("p (c t) -> p c t", c=2)
                if c0 == 0 or OPT['copy_act']:
                    op('act', ACT(uT[:, cc:cc + 2, :], bv, AF.Copy), reads=[bkey], writes=[('uT', cc), ('uT', cc + 1)])
                else:
                    op('dve', CP(uT[:, cc:cc + 2, :], bv), reads=[bkey], writes=[('uT', cc), ('uT', cc + 1)])
                yield
        s = use_piece(i, li, 'wp')
        Wp = W[s][:, 0:2048].rearrange("p (g kc o) -> p g kc o", g=4, kc=2)
        for b in range(NB):
            for gp in range(2):
                bank, bkey = next_bank()
                bkeys = [bkey]
                for gg in range(2):
                    g = gp * 2 + gg
                    for kc in range(2):
                        op('pe', MM(bank[:, gg * 256:(gg + 1) * 256], uT[:, 2 * g + kc, b * 128:(b + 1) * 128],
                                    Wp[:, g, kc, :], kc == 0, kc == 1),
                           reads=[('W', s), ('uT', 2 * g + kc)], writes=bkeys)
                if gp == 0 or OPT['copy_act']:
                    op('act', ACT(m_sb[:, b, gp * 512:(gp + 1) * 512], bank, AF.Copy), reads=bkeys, writes=[('m', b, gp)])
                else:
                    op('dve', CP(m_sb[:, b, gp * 512:(gp + 1) * 512], bank), reads=bkeys, writes=[('m', b, gp)])
            yield
        for j in range(2):
            s = use_piece(i, li, 'gb%d' % j)
            for c0 in (0, 2):
                oc0 = j * 4 + c0
                g = oc0 // 2
                bank, bkey = fm_proj2(s, c0, lambda kc: xT[:, kc, :], xT_keys)
                op('act', ACT(sgb[:, oc0:oc0 + 2, :], bank.rearrange("p (c t) -> p c t", c=2), AF.Silu),
                   reads=[bkey], writes=[('sgb', oc0), ('sgb', oc0 + 1)])
                bank2, bkey2 = next_bank()
                for cc in range(2):
                    oc = oc0 + cc
                    for b in range(NB):
                        first = first_tile and b == 0
                        o_sl = bank2[:, cc * T + b * 128:cc * T + (b + 1) * 128]
                        cur = m_sb[:, b, oc * 128:(oc + 1) * 128]
                        op('pe', MM(o_sl, cur, pm[:, (8 + g) if first else g, :], True, first),
                           reads=[('m', b, oc // 4), 'pm'], writes=[bkey2])
                        if not first:
                            if b > 0:
                                prv, pkey = m_sb[:, b - 1, oc * 128:(oc + 1) * 128], ('m', b - 1, oc // 4)
                            else:
                                prv, pkey = mprev[:, li, oc * 128:(oc + 1) * 128], ('mprev', li)
                            op('pe', MM(o_sl, prv, pm[:, 4 + g, :], False, True), reads=[pkey, 'pm'], writes=[bkey2])
                for cc in range(2):
                    oc = oc0 + cc
                    op('dve', STT(sgb[:, oc, :], bank2[:, cc * T:(cc + 1) * T], psc[:, li, oc:oc + 1], sgb[:, oc, :], ALU.mult, ALU.mult),
                       reads=[bkey2, ('psc', li), ('sgb', oc)], writes=[('sgb', oc)])
                yield
        op('pool', CP(mprev[:, li, :], m_sb[:, NB - 1, :]), reads=[('m', NB - 1, 0), ('m', NB - 1, 1)], writes=[('mprev', li)])
        for j in range(4):
            s = use_piece(i, li, 'ml%d' % j)
            for c0 in (0, 2):
                bank, bkey = fm_proj2(s, c0, lambda kc: xT[:, kc, :], xT_keys)
                for cc in range(2):
                    gi = j * 4 + c0 + cc
                    op('act', ACT(gate[:, gi, :], bank[:, cc * T:(cc + 1) * T], AF.Sigmoid, bias=bmg[:, li, gi:gi + 1], scale=1.0),
                       reads=[bkey, ('bmg', li)], writes=[('gate', gi)])
                yield

    def phase_D(i, li, par):
        l = layers[li]
        last_layer = (li == NL - 1)
        op('sp', DMA(lngb[:, 0, :], lng_in[l:l + 1, :].partition_broadcast(128)), writes=[('lngb', 0)], dma_slot='lngb0')
        op('sp', DMA(lngb[:, 1, :], lnb_in[l:l + 1, :].partition_broadcast(128)), writes=[('lngb', 1)], dma_slot='lngb1')
        sga_keys = [('sga', vc) for vc in range(8)]
        sgb_keys = [('sgb', vc) for vc in range(8)]
        first_nm, second_nm = ('pb', 'pa') if OPT['pb_first'] else ('pa', 'pb')
        src = {'pa': (sga, sga_keys, 0), 'pb': (sgb, sgb_keys, 8)}
        f_buf, f_keys, f_g = src[first_nm]
        s_buf, s_keys, s_g = src[second_nm]
        for j in range(2):
            s = use_piece(i, li, '%s%d' % (first_nm, j))
            for c0 in (0, 2):
                dc = j * 4 + c0
                bank, bkey = fm_proj2(s, c0, lambda kc: f_buf[:, kc, :], f_keys)
                op('dve', TT(tA[:, dc:dc + 2, :], bank.rearrange("p (c t) -> p c t", c=2), gate[:, f_g + dc:f_g + dc + 2, :], ALU.mult),
                   reads=[bkey, ('gate', f_g + dc), ('gate', f_g + dc + 1)], writes=[('tA', dc), ('tA', dc + 1)])
            yield
        while late_tail:
            late_tail.pop(0)()
        for j in range(2):
            s = use_piece(i, li, '%s%d' % (second_nm, j))
            for c0 in (0, 2):
                dc = j * 4 + c0
                bank, bkey = fm_proj2(s, c0, lambda kc: s_buf[:, kc, :], s_keys)
                tb = tB[(dc // 2) % 2]
                tkey = ('tB', (dc // 2) % 2)
                op('dve', TT(tb, bank.rearrange("p (c t) -> p c t", c=2), gate[:, s_g + dc:s_g + dc + 2, :], ALU.mult),
                   reads=[bkey, ('gate', s_g + dc), ('gate', s_g + dc + 1)], writes=[tkey])
                op(OPT['add_eng'], TT(mergedT[:, dc:dc + 2, :], tb, tA[:, dc:dc + 2, :], ALU.add), reads=[tkey, ('tA', dc), ('tA', dc + 1)],
                   writes=[('mg', dc), ('mg', dc + 1)])
            yield
        mg_keys = [('mg', dc) for dc in range(8)]
        Ybanks = [Obk, Ubk]
        Ykeys = [[('PS', 'O0'), ('PS', 'O1')], [('PS', 'U0'), ('PS', 'U1')]]
        for j in range(2):
            s = use_piece(i, li, 'wo%d' % j)
            Wv = W[s].rearrange("p (kc c) -> p kc c", kc=8)
            for b in range(NB):
                for dc in range(8):
                    op('pe', MM(Ybanks[b % 2][j], mergedT[:, dc, b * 128:(b + 1) * 128], Wv[:, dc, :], dc == 0, dc == 7),
                       reads=[('W', s)] + mg_keys, writes=[Ykeys[b % 2][j]])
            yield
        if not OPT['ln_batch']:
            for b in range(NB):
                Yb = Ybanks[b % 2]
                ykeys = Ykeys[b % 2]
                xr = xres[par][b]
                xk = ('xres', par, b)
                for j in range(2):
                    xh = xr[:, j * 512:(j + 1) * 512]
                    op('dve', STT(xh, xh, ALPHA, Yb[j], ALU.mult, ALU.add), reads=[xk, ykeys[j]], writes=[xk])
                for j in range(2):
                    op('dve', lambda e, j=j, xr=xr: e.bn_stats(out=bst[:, j, :], in_=xr[:, j * 512:(j + 1) * 512]),
                       reads=[xk], writes=['bst'])
                op('dve', lambda e: e.bn_aggr(out=mv, in_=bst.rearrange("p a s -> p (a s)")), reads=['bst'], writes=['mv'])
                op('dve', TS(vpe, mv[:, 1:2], EPS, None, ALU.add), reads=['mv'], writes=['vpe'])
                op('pool', TT(rstd1, vpe, nhalf[:, 0:1], ALU.pow), reads=['vpe', 'nhalf'], writes=['rstd1'])
                op('dve', TS(xr, xr, mv[:, 0:1], rstd1, ALU.subtract, ALU.mult), reads=[xk, 'mv', 'rstd1'], writes=[xk])
                op('pool', TT(xr, xr, lngb[:, 0, :], ALU.mult), reads=[xk, ('lngb', 0)], writes=[xk])
                op('pool', TT(xr, xr, lngb[:, 1, :], ALU.add), reads=[xk, ('lngb', 1)], writes=[xk])
                if last_layer:
                    r0 = i * T + b * 128
                    out_ops.append(op('sp', DMA(out[r0:r0 + 128, :], xr), reads=[xk], dma_slot='out%d_%d' % (par, b)))
                elif OPT['defer']:
                    deferred.append((par, b))
                else:
                    build_xT(par, b)
                yield
        else:
            for b in range(NB):
                Yb = Ybanks[b % 2]
                ykeys = Ykeys[b % 2]
                xr = xres[par][b]
                xk = ('xres', par, b)
                for j in range(2):
                    xh = xr[:, j * 512:(j + 1) * 512]
                    op('dve', STT(xh, xh, ALPHA, Yb[j], ALU.mult, ALU.add), reads=[xk, ykeys[j]], writes=[xk])
                for j in range(2):
                    op('dve', lambda e, j=j, xr=xr, b=b: e.bn_stats(out=bst2[:, b, j, :], in_=xr[:, j * 512:(j + 1) * 512]),
                       reads=[xk], writes=[('bst', b)])
                op('dve', lambda e, b=b: e.bn_aggr(out=mv2[:, b, :], in_=bst2[:, b, :, :].rearrange("p a s -> p (a s)")),
                   reads=[('bst', b)], writes=[('mv', b)])
            op('dve', TS(vpe2, mv2[:, :, 1], EPS, None, ALU.add), reads=[('mv', b) for b in range(NB)], writes=['vpe2'])
            op('pool', TT(rstd2, vpe2, nhalf[:, 0:NB], ALU.pow), reads=['vpe2', 'nhalf'], writes=['rstd2'])
            yield
            for b in range(NB):
                xr = xres[par][b]
                xk = ('xres', par, b)
                op('dve', TS(xr, xr, mv2[:, b, 0:1], rstd2[:, b:b + 1], ALU.subtract, ALU.mult), reads=[xk, ('mv', b), 'rstd2'], writes=[xk])
            for b in range(NB):
                xr = xres[par][b]
                xk = ('xres', par, b)
                op('pool', TT(xr, xr, lngb[:, 0, :], ALU.mult), reads=[xk, ('lngb', 0)], writes=[xk])
                op('pool', TT(xr, xr, lngb[:, 1, :], ALU.add), reads=[xk, ('lngb', 1)], writes=[xk])
                if last_layer:
                    r0 = i * T + b * 128
                    out_ops.append(op('sp', DMA(out[r0:r0 + 128, :], xr), reads=[xk], dma_slot='out%d_%d' % (par, b)))
                else:
                    build_xT(par, b)
            yield

    def load_x(i):
        par = i % 2
        for b in range(NB):
            r0 = i * T + b * 128
            op('sp', DMA(xres[par][b], x_in[r0:r0 + 128, :]), writes=[('xres', par, b)], dma_slot='xin%d_%d' % (par, b))

    def run_all(*gens):
        for g in gens:
            for _ in g:
                pass

    def run_interleaved(ga, gb, ratio=None):
        ratio = ratio or OPT['ratio']
        done_a = done_b = False
        for _ in range(OPT['c_first']):
            if not done_b:
                try:
                    next(gb)
                except StopIteration:
                    done_b = True
        while not (done_a and done_b):
            if not done_a:
                try:
                    next(ga)
                except StopIteration:
                    done_a = True
            for _ in range(ratio):
                if not done_b:
                    try:
                        next(gb)
                    except StopIteration:
                        done_b = True

    deferred = []

    def flush_deferred():
        pend = list(deferred)
        del deferred[:]
        for (p_, b_) in pend:
            build_xT(p_, b_)

    def tile_layer(i, li, par):
        set_pool(['P0', 'P1', 'P2', 'AB', 'O0', 'O1', 'U0', 'U1'])
        pg.tag = 'PH_A_%d_%d' % (i, li)
        if [d for d in deferred if d[0] == par]:
            flush_deferred()
        if OPT['early_g'] and interleave:
            import itertools
            gA = phase_A(i, li, par)
            for v_ in gA:
                if v_ == 'SPLIT':
                    break
            pg.tag = 'PH_GC_%d_%d' % (i, li)
            set_pool(['P0', 'P1', 'P2', 'AB'])
            gG = phase_G(i, li, par)
            next(gG)
            next(gA)
            next(gA)
            flush_deferred()
            run_interleaved(gG, itertools.chain(gA, phase_C(i, li, par)))
        else:
            run_all(phase_A(i, li, par))
            flush_deferred()
            pg.tag = 'PH_GC_%d_%d' % (i, li)
            set_pool(['P0', 'P1', 'P2', 'AB'] if OPT['ab_alias'] else ['P0', 'P1', 'P2'])
            if interleave:
                run_interleaved(phase_G(i, li, par), phase_C(i, li, par))
            else:
                run_all(phase_G(i, li, par), phase_C(i, li, par))
        set_pool(['P0', 'P1', 'P2', 'AB'])
        pg.tag = 'PH_D_%d_%d' % (i, li)
        run_all(phase_D(i, li, par))

    assert NT % 2 == 0
    set_pool(['P0', 'P1', 'P2', 'AB', 'O0', 'O1', 'U0', 'U1'])
    for m in range(NT // 2):
        tiles = (2 * m, 2 * m + 1)
        for par, i in enumerate(tiles):
            if m == 0:
                load_x(i)
            if m == 0 or not OPT['pair_defer']:
                for b in range(NB):
                    build_xT(par, b)
        for li in range(NL):
            for par, i in enumerate(tiles):
                tile_layer(i, li, par)
                if li == NL - 1 and m + 1 < NT // 2:
                    load_x(i + 2)
                    if OPT['pair_defer']:
                        deferred.extend((par, b) for b in range(NB))

    pg.emit(final_wait_ops=out_ops)
    return nc, pg


PARAM_NAMES = ["w_in", "w_alpha_up", "b_alpha", "gla_norm_g", "w_pool_grp", "pool_scale", "b_merge",
               "w_proj_a", "w_proj_b", "w_out", "ln_g", "ln_b"]


def _prep_params(inputs):
    p = {}
    for k in PARAM_NAMES:
        a = np.ascontiguousarray(np.asarray(inputs[k], dtype=np.float32))
        if k == "gla_norm_g":
            a = a.reshape(DEPTH, 1024)
        p[k] = a
    p.update(make_consts())
    return p


_NC_CACHE = {}


def kernel(**inputs):
    x = np.ascontiguousarray(np.asarray(inputs["x"], dtype=np.float32))
    params = _prep_params(inputs)
    key = (x.shape[1], (0, 1, 2, 3))
    if key not in _NC_CACHE:
        _NC_CACHE[key] = build_nc(x.shape[1], (0, 1, 2, 3))[0]
    nc = _NC_CACHE[key]
    n = 8
    consts = make_consts()
    zero_map = {k: np.zeros_like(v) for k, v in params.items() if k not in consts}
    zero_map.update(consts)
    zero_map["x"] = np.zeros_like(x[0])
    in_maps = []
    for c in range(n):
        if c % 2 == 0:
            m = dict(params)
            m["x"] = x[c // 2]
        else:
            m = zero_map
        in_maps.append(m)
    res = run_bass_kernel_spmd(nc, in_maps, core_ids=list(range(n)))
    return np.stack([res.results[2 * b]["out"] for b in range(BATCH)], axis=0).astype(np.float32)
```
